# Optimizing a Trainium2 kernel written in Bass

```python
import math
import jax
import jax.numpy as jnp
from jax import lax
import numpy as np

D_MODEL = 2048
BATCH = 2
SEQ = 16384
DEPTH = 2

N_MIXERS = 2
N_META = 16
CONV_WIDTH = 31
HEAD_DIM = 64
N_HEADS = D_MODEL // HEAD_DIM
N_KV_HEADS = N_HEADS // 8
GROUP = N_HEADS // N_KV_HEADS
WINDOW = 128
BLOCK = 128
N_BUCKETS = 32
MAX_DISTANCE = 128
D_FF = ((8 * D_MODEL + 3 * 256 - 1) // (3 * 256)) * 256
QKV_WIDTH = (N_HEADS + 2 * N_KV_HEADS) * HEAD_DIM
RMS_EPS = 1e-6
LN_EPS = 1e-5
NEG_INF = -1e30

kernel_name = 'hybrid_conformer_conv_swa_sink_block'


def rms_norm(x, g):
    xf = x.astype(jnp.float32)
    y = xf * lax.rsqrt(jnp.mean(xf * xf, axis=-1, keepdims=True) + RMS_EPS)
    return (y * g.astype(jnp.float32)).astype(x.dtype)


def layer_norm(x, g, b):
    xf = x.astype(jnp.float32)
    mu = jnp.mean(xf, axis=-1, keepdims=True)
    var = jnp.mean(jnp.square(xf - mu), axis=-1, keepdims=True)
    y = (xf - mu) * lax.rsqrt(var + LN_EPS)
    return (y * g.astype(jnp.float32) + b.astype(jnp.float32)).astype(x.dtype)


def t5_bucket(dist):
    max_exact = N_BUCKETS // 2
    d = jnp.maximum(dist, max_exact).astype(jnp.float32)
    large = max_exact + (jnp.log(d / max_exact) / math.log(MAX_DISTANCE / max_exact)
                         * (N_BUCKETS - max_exact)).astype(jnp.int32)
    return jnp.where(dist < max_exact, dist, jnp.minimum(large, N_BUCKETS - 1))


def rel_bias_lookup(rel_bias, dist):
    b = rel_bias.astype(jnp.float32)[t5_bucket(dist)]
    b = jnp.moveaxis(b, -1, -3)
    return b.reshape(b.shape[:-3] + (N_KV_HEADS, GROUP) + b.shape[-2:])


def sink_softmax(scores, sinks):
    sink = jnp.broadcast_to(sinks[None, :, :, None, None], scores.shape[:-1] + (1,))
    p = jax.nn.softmax(jnp.concatenate([scores, sink], axis=-1), axis=-1)
    return p[..., :-1]


def conformer_conv(h, w_in, b_in, w_dw, b_dw, ln_g, ln_b, w_out, b_out):
    u = h @ w_in + b_in
    a, g = jnp.split(u, 2, axis=-1)
    u = a * jax.nn.sigmoid(g)
    u = jnp.pad(u, ((0, 0), (CONV_WIDTH - 1, 0), (0, 0)))
    u = lax.conv_general_dilated(
        u, w_dw[:, None, :].astype(u.dtype), window_strides=(1,), padding='VALID',
        dimension_numbers=('NWC', 'WIO', 'NWC'), feature_group_count=D_MODEL) + b_dw
    u = layer_norm(u, ln_g, ln_b)
    u = jax.nn.silu(u)
    return u @ w_out + b_out


def sliding_window_sink_attention(h, w_qkv, b_qkv, sinks, w_o, b_o, rel_bias):
    B, L, _ = h.shape
    S = L - N_META
    n_blk = S // BLOCK
    qkv = h @ w_qkv + b_qkv
    q, k, v = jnp.split(qkv, [N_HEADS * HEAD_DIM, (N_HEADS + N_KV_HEADS) * HEAD_DIM], axis=-1)
    q = q.reshape(B, L, N_KV_HEADS, GROUP, HEAD_DIM) * (HEAD_DIM ** -0.5)
    k = k.reshape(B, L, N_KV_HEADS, HEAD_DIM)
    v = v.reshape(B, L, N_KV_HEADS, HEAD_DIM)
    sink_logits = sinks.astype(jnp.float32).reshape(N_KV_HEADS, GROUP)
    q_m, q_r = q[:, :N_META], q[:, N_META:]
    k_m, k_r = k[:, :N_META], k[:, N_META:]
    v_m, v_r = v[:, :N_META], v[:, N_META:]

    d_mm = jnp.arange(N_META)[:, None] - jnp.arange(N_META)[None, :]
    s_mm = (jnp.einsum('bqhgd,bkhd->bhgqk', q_m, k_m).astype(jnp.float32)
            + rel_bias_lookup(rel_bias, jnp.maximum(d_mm, 0)))
    s_mm = jnp.where(d_mm >= 0, s_mm, NEG_INF)
    p_mm = sink_softmax(s_mm, sink_logits).astype(v.dtype)
    o_m = jnp.einsum('bhgqk,bkhd->bqhgd', p_mm, v_m)

    pad = ((0, 0), (BLOCK, 0), (0, 0), (0, 0))
    k_p = jnp.pad(k_r, pad).reshape(B, n_blk + 1, BLOCK, N_KV_HEADS, HEAD_DIM)
    v_p = jnp.pad(v_r, pad).reshape(B, n_blk + 1, BLOCK, N_KV_HEADS, HEAD_DIM)
    k_band = jnp.concatenate([k_p[:, :-1], k_p[:, 1:]], axis=2)
    v_band = jnp.concatenate([v_p[:, :-1], v_p[:, 1:]], axis=2)
    q_blk = q_r.reshape(B, n_blk, BLOCK, N_KV_HEADS, GROUP, HEAD_DIM)

    qpos = jnp.arange(BLOCK)[:, None]
    kpos = jnp.arange(2 * BLOCK)[None, :]
    mpos = jnp.arange(N_META)[None, :]
    d_band = BLOCK + qpos - kpos
    in_window = (d_band >= 0) & (d_band < WINDOW)
    bias_band = rel_bias_lookup(rel_bias, jnp.maximum(d_band, 0))

    def block_fn(args):
        n, q_n, k_n, v_n = args
        valid = in_window & (n * BLOCK + kpos >= BLOCK)
        d_meta = N_META + n * BLOCK + qpos - mpos
        s_meta = (jnp.einsum('bqhgd,bmhd->bhgqm', q_n, k_m).astype(jnp.float32)
                  + rel_bias_lookup(rel_bias, d_meta))
        s_band = jnp.einsum('bqhgd,bjhd->bhgqj', q_n, k_n).astype(jnp.float32) + bias_band
        s_band = jnp.where(valid, s_band, NEG_INF)
        p = sink_softmax(jnp.concatenate([s_meta, s_band], axis=-1), sink_logits).astype(v_n.dtype)
        return (jnp.einsum('bhgqm,bmhd->bqhgd', p[..., :N_META], v_m)
                + jnp.einsum('bhgqj,bjhd->bqhgd', p[..., N_META:], v_n))

    xs = (jnp.arange(n_blk, dtype=jnp.int32), jnp.moveaxis(q_blk, 1, 0),
          jnp.moveaxis(k_band, 1, 0), jnp.moveaxis(v_band, 1, 0))
    o_r = lax.map(block_fn, xs)
    o_r = jnp.moveaxis(o_r, 0, 1).reshape(B, S, N_HEADS * HEAD_DIM)
    o = jnp.concatenate([o_m.reshape(B, N_META, N_HEADS * HEAD_DIM), o_r], axis=1)
    return o @ w_o + b_o


def swiglu(h, w_gate, w_up, w_down):
    return (jax.nn.silu(h @ w_gate) * (h @ w_up)) @ w_down


def setup_inputs(seed: int = 0) -> dict:
    key = jax.random.key(seed)
    ks = list(jax.random.split(key, 24))
    n_conv = len(range(0, DEPTH, N_MIXERS))
    n_attn = len(range(1, DEPTH, N_MIXERS))
    D, F, H = D_MODEL, D_FF, N_HEADS * HEAD_DIM

    def nrm(k, shape, scale):
        return jax.random.normal(k, shape, jnp.float32) * scale

    def gain(k, shape):
        return 1.0 + nrm(k, shape, 0.02)

    return {
        'x': nrm(ks[0], (BATCH, SEQ, D), 1.0),
        'meta_tokens': nrm(ks[1], (N_META, D), 1.0),
        'rel_bias': nrm(ks[2], (N_BUCKETS, N_HEADS), 0.5),
        'conv_w_in': nrm(ks[3], (n_conv, D, 2 * D), D ** -0.5),
        'conv_b_in': nrm(ks[4], (n_conv, 2 * D), 0.02),
        'conv_w_dw': nrm(ks[5], (n_conv, CONV_WIDTH, D), CONV_WIDTH ** -0.5),
        'conv_b_dw': nrm(ks[6], (n_conv, D), 0.02),
        'conv_ln_g': gain(ks[7], (n_conv, D)),
        'conv_ln_b': nrm(ks[8], (n_conv, D), 0.02),
        'conv_w_out': nrm(ks[9], (n_conv, D, D), D ** -0.5),
        'conv_b_out': nrm(ks[10], (n_conv, D), 0.02),
        'attn_w_qkv': nrm(ks[11], (n_attn, D, QKV_WIDTH), D ** -0.5),
        'attn_b_qkv': nrm(ks[12], (n_attn, QKV_WIDTH), 0.02),
        'attn_sinks': nrm(ks[13], (n_attn, N_HEADS), 0.5),
        'attn_w_o': nrm(ks[14], (n_attn, H, D), H ** -0.5),
        'attn_b_o': nrm(ks[15], (n_attn, D), 0.02),
        'norm_mix_pre': gain(ks[16], (DEPTH, D)),
        'norm_mix_post': gain(ks[17], (DEPTH, D)),
        'norm_ffn_pre': gain(ks[18], (DEPTH, D)),
        'norm_ffn_post': gain(ks[19], (DEPTH, D)),
        'ffn_w_gate': nrm(ks[20], (DEPTH, D, F), D ** -0.5),
        'ffn_w_up': nrm(ks[21], (DEPTH, D, F), D ** -0.5),
        'ffn_w_down': nrm(ks[22], (DEPTH, F, D), F ** -0.5),
    }


def reference(x, meta_tokens, rel_bias, conv_w_in, conv_b_in, conv_w_dw, conv_b_dw,
              conv_ln_g, conv_ln_b, conv_w_out, conv_b_out, attn_w_qkv, attn_b_qkv,
              attn_sinks, attn_w_o, attn_b_o, norm_mix_pre, norm_mix_post,
              norm_ffn_pre, norm_ffn_post, ffn_w_gate, ffn_w_up, ffn_w_down):
    B = x.shape[0]
    meta = jnp.broadcast_to(meta_tokens.astype(x.dtype)[None], (B, N_META, D_MODEL))
    h = jnp.concatenate([meta, x], axis=1)
    for i in range(DEPTH):
        j = i // N_MIXERS
        u = rms_norm(h, norm_mix_pre[i])
        if i % N_MIXERS == 0:
            u = conformer_conv(u, conv_w_in[j], conv_b_in[j], conv_w_dw[j], conv_b_dw[j],
                               conv_ln_g[j], conv_ln_b[j], conv_w_out[j], conv_b_out[j])
        else:
            u = sliding_window_sink_attention(u, attn_w_qkv[j], attn_b_qkv[j], attn_sinks[j],
                                              attn_w_o[j], attn_b_o[j], rel_bias)
        h = h + rms_norm(u, norm_mix_post[i])
        u = swiglu(rms_norm(h, norm_ffn_pre[i]), ffn_w_gate[i], ffn_w_up[i], ffn_w_down[i])
        h = h + rms_norm(u, norm_ffn_post[i])
    return h[:, N_META:]
```

```python
import math
from contextlib import ExitStack

import numpy as np
import concourse.bass as bass
import concourse.mybir as mybir
from concourse.bass_utils import run_bass_kernel_spmd

F32 = mybir.dt.float32
BF16 = mybir.dt.bfloat16
AF = mybir.ActivationFunctionType
ALU = mybir.AluOpType

D = 2048
NCH = 16
DFF = 5632
NF = 44
NFH = 22
NMETA = 16
HALO = 176
T0 = NMETA + HALO
TW = 512
CW = 31
NHEADS = 32
RMS_EPS = 1e-6
LN_EPS = 1e-5
NEG = -30000.0
NSLOT = 4
NPE = 15
SLOT_EL = 4096

_cols = {}
_n = 0
for _name, _w in [("g_mix_pre0", 16), ("g_mix_post0", 16), ("g_ffn_pre0", 16), ("g_ffn_post0", 16),
                  ("g_mix_pre1", 16), ("g_mix_post1", 16), ("g_ffn_pre1", 16), ("g_ffn_post1", 16),
                  ("b_in_a", 16), ("b_in_g", 16), ("b_dw", 16), ("ln_g", 16), ("ln_b", 16),
                  ("b_out", 16), ("b_q", 16), ("b_o", 16), ("b_k", 4), ("sinks", 16),
                  ("w_dw", CW * 16), ("flag", 1), ("bq_s", 16), ("es", 16)]:
    _cols[_name] = _n
    _n += _w
NCV = _n


class Op:
    __slots__ = ("eng", "fn", "deps", "signal", "sigval", "seg", "pos", "is_dma", "dkey", "dval", "idx")


class Sched:
    ENGS = ("pe", "act", "dve", "pool", "sp")

    def __init__(self):
        self.ops = []
        self.by_eng = {e: [] for e in self.ENGS}
        self.last_w = {}
        self.readers = {}
        self.seg = 0
        self.dma_count = {}

    def add(self, eng, fn, reads=(), writes=(), dma=None, nodep_writes=()):
        op = Op()
        op.eng = eng
        op.fn = fn
        op.signal = False
        op.sigval = 0
        op.seg = self.seg
        op.is_dma = dma is not None
        op.dkey = dma
        op.idx = len(self.ops)
        deps = {}
        for r in reads:
            w = self.last_w.get(r)
            if w is not None:
                deps[w.idx] = w
        for r in writes:
            w = self.last_w.get(r)
            if w is not None:
                deps[w.idx] = w
            for o in self.readers.get(r, {}).values():
                deps[o.idx] = o
        for r in reads:
            self.readers.setdefault(r, {})[eng if not op.is_dma else ("dma", op.idx)] = op
        for r in list(writes) + list(nodep_writes):
            self.last_w[r] = op
            self.readers[r] = {}
        deps.pop(op.idx, None)
        op.deps = list(deps.values())
        if op.is_dma:
            self.dma_count[dma] = self.dma_count.get(dma, 0) + 1
            op.dval = 16 * self.dma_count[dma]
        op.pos = len(self.by_eng[eng])
        self.by_eng[eng].append(op)
        self.ops.append(op)
        return op

    def _needs_wait(self, op, d):
        if d.is_dma:
            return True
        if d.eng == op.eng:
            if op.is_dma:
                return True
            if op.eng == "pe":
                return False
            return (op.pos - d.pos) <= 1
        return True

    def emit(self, nc, es):
        for op in self.ops:
            for d in op.deps:
                if not d.is_dma and self._needs_wait(op, d):
                    d.signal = True
        cnt = {}
        for op in self.ops:
            if op.signal and not op.is_dma:
                k = (op.eng, op.seg)
                cnt[k] = cnt.get(k, 0) + 1
                op.sigval = cnt[k]
        esem = {k: es.enter_context(nc.semaphore("s_%s_%d" % k)) for k in cnt}
        dsem = {}
        for i, k in enumerate(self.dma_count):
            dsem[k] = es.enter_context(nc.semaphore("d_%d" % i))
        block = es.enter_context(nc.Block())

        def run(e, name):
            waited = {}
            for op in self.by_eng[name]:
                for d in op.deps:
                    if not self._needs_wait(op, d):
                        continue
                    if d.is_dma:
                        sem, val = dsem[d.dkey], d.dval
                    else:
                        sem, val = esem[(d.eng, d.seg)], d.sigval
                    key = id(sem)
                    if waited.get(key, 0) >= val:
                        continue
                    e.wait_ge(sem, val)
                    waited[key] = val
                ins = op.fn(e)
                if ins is None:
                    continue
                if op.is_dma:
                    ins.then_inc(dsem[op.dkey], 16)
                elif op.signal:
                    ins.then_inc(esem[(op.eng, op.seg)], 1)

        @block.tensor
        def _(e):
            run(e, "pe")

        @block.scalar
        def _(e):
            run(e, "act")

        @block.vector
        def _(e):
            run(e, "dve")

        @block.gpsimd
        def _(e):
            run(e, "pool")

        @block.sync
        def _(e):
            run(e, "sp")


def build_program(NT):
    NTOK = T0 + NT * TW
    nc = bass.Bass("TRN2", target_bir_lowering=False)
    S = Sched()

    def din(name, shape, dt=F32):
        return nc.dram_tensor(name, list(shape), dt, kind="ExternalInput").ap()

    xin = din("xin", [128, NCH, NTOK])
    cv_d = din("cv", [128, NCV])
    bv_d = din("bv_rep", [128, 256])
    mask_d = din("mask0", [128, T0])
    mg_d = din("mg", [16, 32])
    bprev_d = din("bias_prev", [128, 4096])
    bcur_d = din("bias_cur", [128, 4096])
    ident_d = din("ident", [128, 128])
    mef_d = din("mef", [16, 2, NCH, 128])
    out_d = nc.dram_tensor("out", [128, NCH, NT * TW], F32, kind="ExternalOutput").ap()

    streams = {"cin": (16, 4096), "dg": (16, NPE * 128), "cout": (8, 4096), "gu0": (NF, 4096), "dn0": (32, 2816),
               "qkv": (11, 4096), "wo": (8, 4096), "gu1": (NF, 4096), "dn1": (32, 2816)}
    w32 = {}
    wbf = {}
    for k, (nu, x) in streams.items():
        w32[k] = din("w_" + k, [nu, 128, x])
        wbf[k] = nc.dram_tensor("wb_" + k, [nu, 128, x], BF16, kind="Internal").ap()

    with ExitStack() as es:
        def sb(name, shape, dt):
            return es.enter_context(nc.sbuf_tensor(name, list(shape), dt))

        h = sb("h", [128, NCH, TW], F32)
        xn = sb("xn", [128, NCH, TW], BF16)
        v = sb("v", [128, NCH, TW], F32)
        big = sb("big", [128, NFH, TW], BF16)
        ring = sb("ring", [128, NSLOT, SLOT_EL], BF16)
        kbuf = sb("kbuf", [128, 4, 8 * 128], BF16)
        vbuf = sb("vbuf", [128, 8, 256], BF16)
        kmeta = sb("kmeta", [128, 4, NMETA], BF16)
        vmeta = sb("vmeta", [48, 256], BF16)
        expb_p = sb("expb_p", [128, 2, NCH, 128], BF16)
        expb_c = sb("expb_c", [128, 2, NCH, 128], BF16)
        ident = sb("ident_t", [128, 128], BF16)
        cv = sb("cvt", [128, NCV], F32)
        bv = sb("bvt", [128, 256], F32)
        mask0 = sb("mask0t", [128, T0], F32)
        mg = sb("mgt", [48, 32], F32)
        ones = sb("ones", [128, 128], BF16)
        ucarry = sb("ucarry", [128, NCH, CW - 1], BF16)
        NU = 6
        ub = [sb("ub%d" % i, [128, CW - 1 + TW], BF16) for i in range(NU)]
        NSQ = 4
        sqb = [sb("sqb%d" % i, [128, TW], BF16) for i in range(NSQ)]
        ybfb = [sb("ybf%d" % i, [128, TW], BF16) for i in range(2)]
        NFB = 4
        fb = [sb("fb%d" % i, [128, TW], F32) for i in range(NFB)]
        st1 = sb("st1", [128, TW], F32)
        st2 = sb("st2", [128, TW], F32)
        st3 = sb("st3", [128, TW], F32)
        NP = 8
        pb = [sb("pb%d" % i, [128, TW], BF16) for i in range(NP)]
        ps = [es.enter_context(nc.psum_tensor("ps%d" % i, [128, TW], F32)) for i in range(8)]

        def col(name, c=0, n=1):
            o = _cols[name] + c
            return cv[:, o:o + n]

        cnt = {"u": 0, "sq": 0, "fb": 0, "pb": 0, "pm": 0, "ybf": 0, "ub": 0}

        def nxt(k, n):
            i = cnt[k] % n
            cnt[k] += 1
            return i

        S.seg = 0
        S.add("sp", lambda e: e.dma_start(out=cv[:], in_=cv_d), writes=["cv"], dma="c_cv")
        S.add("sp", lambda e: e.dma_start(out=bv[:], in_=bv_d), writes=["bv"], dma="c_bv")
        S.add("sp", lambda e: e.dma_start(out=mask0[:], in_=mask_d), writes=["mask0"], dma="c_mask")
        S.add("sp", lambda e: e.dma_start(out=mg[0:16, :], in_=mg_d), nodep_writes=["mg"], dma="c_mg")
        S.add("sp", lambda e: e.dma_start(out=mg[32:48, :], in_=mg_d), nodep_writes=["mg"], dma="c_mg")
        ebp = expb_p[:].rearrange("p a c q -> p (a c q)")
        ebc = expb_c[:].rearrange("p a c q -> p (a c q)")
        S.add("pool", lambda e: e.dma_start(out=ebp, in_=bprev_d), writes=["expb_p"], dma="c_bp")
        S.add("pool", lambda e: e.dma_start(out=ebc, in_=bcur_d), writes=["expb_c"], dma="c_bc")
        S.add("pool", lambda e: e.dma_start(out=ident[:], in_=ident_d), writes=["ident"], dma="c_id")
        LATE = ("wo", "gu1", "dn1")
        FINE = ("cin",)

        def emit_casts(keys, extra_reads=()):
            for k in keys:
                nu = streams[k][0]
                for u in range(nu):
                    fine = k in FINE
                    S.add("pool", (lambda e, k=k, u=u: e.dma_start(out=wbf[k][u], in_=w32[k][u])),
                          reads=list(extra_reads), nodep_writes=[("wbf", k, u) if fine else ("wbf", k)],
                          dma=("cast", k, u) if fine else ("cast", k))

        emit_casts([k for k in streams if k not in LATE])
        S.add("dve", lambda e: e.memset(ones[:], 1.0), writes=["ones"])
        S.add("dve", lambda e: e.memset(ucarry[:], 0.0), writes=[("ucarry", c) for c in range(NCH)])
        S.add("dve", lambda e: e.tensor_scalar(out=col("bq_s", 0, 16), in0=col("b_q", 0, 16), scalar1=0.125,
                                               scalar2=None, op0=ALU.mult), reads=["cv"], writes=["cv2"])
        S.add("act", lambda e: e.activation(out=col("es", 0, 16), in_=col("sinks", 0, 16), func=AF.Exp),
              reads=["cv"], writes=["cv3"])
        S.add("act", lambda e: e.activation(out=mg[0:16, :], in_=mg[0:16, :], func=AF.Exp), reads=["mg"], writes=["mg"])
        S.add("act", lambda e: e.activation(out=mg[32:48, :], in_=mg[32:48, :], func=AF.Exp), reads=["mg"], writes=["mg"])
        CONST = ["cv", "cv2", "cv3"]

        unit_ctr = [0]

        def load_unit(stream, u):
            slot = unit_ctr[0] % NSLOT
            unit_ctr[0] += 1
            x = streams[stream][1]
            S.add("sp", lambda e: e.dma_start(out=ring[:, slot, 0:x], in_=wbf[stream][u]),
                  reads=[("wbf", stream, u) if stream in FINE else ("wbf", stream)], writes=[("ring", slot)],
                  dma=("ring", slot))
            return slot

        def mm(bank, lhsT, rhs, start, stop, reads, out_ap=None):
            o = ps[bank][:] if out_ap is None else out_ap
            S.add("pe", lambda e: e.matmul(o, lhsT=lhsT, rhs=rhs, start=start, stop=stop),
                  reads=reads, writes=[("ps", bank)])

        def act(out, in_, func, reads, writes, bias=None, scale=None):
            kw = {}
            if bias is not None:
                kw["bias"] = bias
            if scale is not None:
                kw["scale"] = scale
            S.add("act", lambda e: e.activation(out=out, in_=in_, func=func, **kw),
                  reads=list(reads) + CONST, writes=writes)

        pending_stat = []

        def flush_stat():
            while pending_stat:
                pending_stat.pop(0)()

        def stat_mm(bank, sqi, tw, first, last, src_key):
            def f():
                mm(bank, ones[:], sqb[sqi][:, 0:tw] if src_key == "sq" else ybfb[sqi][:, 0:tw], first, last,
                   ["ones", (src_key, sqi)], out_ap=ps[bank][:, 0:tw])
            return f

        def rstd_from(bank, tw, eps):
            act(st1[:, 0:tw], ps[bank][:, 0:tw], AF.Ln, [("ps", bank)], ["st1", ("ps", bank)],
                bias=eps_t[eps], scale=1.0 / D)
            act(st2[:, 0:tw], st1[:, 0:tw], AF.Exp, ["st1"], ["st2"], scale=-0.5)

        eps_t = {}
        for nm, val in (("rms", RMS_EPS), ("ln", LN_EPS)):
            t_ = sb("eps_" + nm, [128, 1], F32)
            S.add("dve", (lambda e, t_=t_, val=val: e.memset(t_[:], val)), writes=["eps_" + nm])
            eps_t[nm] = t_[:]
            CONST.append("eps_" + nm)

        def rmsnorm_to_xn(gname, tw, split=False):
            for c in range(NCH):
                i = nxt("sq", NSQ)
                if c % 2 == 0 or not split:
                    act(sqb[i][:, 0:tw], h[:, c, 0:tw], AF.Square, [("h", c)], [("sq", i)])
                else:
                    S.add("dve", (lambda e, i=i, c=c: e.tensor_tensor(out=sqb[i][:, 0:tw], in0=h[:, c, 0:tw],
                                                                      in1=h[:, c, 0:tw], op=ALU.mult)),
                          reads=[("h", c)], writes=[("sq", i)])
                mm(7, ones[:], sqb[i][:, 0:tw], c == 0, c == NCH - 1, ["ones", ("sq", i)], out_ap=ps[7][:, 0:tw])
            rstd_from(7, tw, "rms")
            for c in range(NCH):
                S.add("dve", (lambda e, c=c: e.scalar_tensor_tensor(out=xn[:, c, 0:tw], in0=h[:, c, 0:tw],
                                                                    scalar=col(gname, c), in1=st2[:, 0:tw],
                                                                    op0=ALU.mult, op1=ALU.mult)),
                      reads=[("h", c), "st2"] + CONST, writes=[("xn", c)])

        def post_norm_residual(gname, tw):
            flush_stat()
            rstd_from(7, tw, "rms")
            for c in range(NCH):
                i = nxt("fb", NFB)
                S.add("dve", (lambda e, c=c, i=i: e.scalar_tensor_tensor(out=fb[i][:, 0:tw], in0=v[:, c, 0:tw],
                                                                         scalar=col(gname, c), in1=st2[:, 0:tw],
                                                                         op0=ALU.mult, op1=ALU.mult)),
                      reads=[("v", c), "st2"] + CONST, writes=[("fb", i)])
                S.add("dve", (lambda e, c=c, i=i: e.tensor_tensor(out=h[:, c, 0:tw], in0=h[:, c, 0:tw],
                                                                  in1=fb[i][:, 0:tw], op=ALU.add)),
                      reads=[("h", c), ("fb", i)], writes=[("h", c)])

        def evac_v_and_stat(bank, m, tw, bias_ap, first, last):
            if bias_ap is not None:
                act(v[:, m, 0:tw], ps[bank][:, 0:tw], AF.Identity, [("ps", bank)], [("v", m), ("ps", bank)], bias=bias_ap)
            else:
                act(v[:, m, 0:tw], ps[bank][:, 0:tw], AF.Copy, [("ps", bank)], [("v", m), ("ps", bank)])
            i = nxt("sq", NSQ)
            act(sqb[i][:, 0:tw], v[:, m, 0:tw], AF.Square, [("v", m)], [("sq", i)])
            flush_stat()
            pending_stat.append(stat_mm(7, i, tw, first, last, "sq"))

        def proj_fm(stream, u0, n_out, rhs_of, rhs_res, tw, evac, cols_per_unit=2, kouter=False):
            slot = None
            m0 = 0
            if kouter and n_out >= 4:
                slots = [load_unit(stream, u0), load_unit(stream, u0 + 1)]
                for k in range(NCH):
                    for m in range(4):
                        sl = slots[m // 2]
                        wv = ring[:, sl, :].rearrange("p (k c) -> p k c", k=NCH)
                        lo = (m % 2) * 128
                        mm(m, wv[:, k, lo:lo + 128], rhs_of(k), k == 0, k == NCH - 1,
                           [("ring", sl), rhs_res(k)], out_ap=ps[m][:, 0:tw])
                for m in range(4):
                    evac(m, m)
                m0 = 4
            for m in range(m0, n_out):
                if m % cols_per_unit == 0:
                    slot = load_unit(stream, u0 + m // cols_per_unit)
                wv = ring[:, slot, :].rearrange("p (k c) -> p k c", k=NCH)
                lo = (m % cols_per_unit) * 128
                bank = m % 4
                for k in range(NCH):
                    mm(bank, wv[:, k, lo:lo + 128], rhs_of(k), k == 0, k == NCH - 1,
                       [("ring", slot), rhs_res(k)], out_ap=ps[bank][:, 0:tw])
                evac(bank, m)

        def conv_mixer(t, tw):
            rmsnorm_to_xn("g_mix_pre0", tw, split=True)

            def taps(pair):
                for (c, ui) in pair:
                    slot = load_unit("dg", c)
                    dv = ring[:, slot, 0:NPE * 128].rearrange("p (j m) -> p j m", j=NPE)
                    bank = 4 + c % 2
                    for jj in range(NPE):
                        j = CW - NPE + jj
                        mm(bank, dv[:, jj, :], ub[ui][:, j:j + tw], jj == 0, jj == NPE - 1, [("ring", slot), ("ub", ui)],
                           out_ap=ps[bank][:, 0:tw])
                    act(v[:, c, 0:tw], ps[bank][:, 0:tw], AF.Identity, [("ps", bank)], [("v", c), ("ps", bank)],
                        bias=col("b_dw", c))
                for j in range(CW - NPE):
                    for (c, ui) in pair:
                        wcol = col("w_dw", j * 16 + c)
                        S.add("dve", (lambda e, c=c, ui=ui, wcol=wcol, j=j: e.scalar_tensor_tensor(
                            out=v[:, c, 0:tw], in0=ub[ui][:, j:j + tw], scalar=wcol, in1=v[:, c, 0:tw],
                            op0=ALU.mult, op1=ALU.add)),
                            reads=[("ub", ui), ("v", c)] + CONST, writes=[("v", c)])
                for (c, ui) in pair:
                    yi = nxt("ybf", 2)
                    act(ybfb[yi][:, 0:tw], v[:, c, 0:tw], AF.Copy, [("v", c)], [("ybf", yi)])
                    si = nxt("sq", NSQ)
                    act(sqb[si][:, 0:tw], v[:, c, 0:tw], AF.Square, [("v", c)], [("sq", si)])
                    pending_stat.append(stat_mm(6, yi, tw, c == 0, c == NCH - 1, "ybf"))
                    pending_stat.append(stat_mm(7, si, tw, c == 0, c == NCH - 1, "sq"))

            prev = None
            cur = []
            sl2 = [load_unit("cin", 0), load_unit("cin", 1)]
            for k in range(NCH):
                for cc in range(2):
                    wv = ring[:, sl2[cc], :].rearrange("p (k c) -> p k c", k=NCH)
                    mm(cc, wv[:, k, 0:128], xn[:, k, 0:tw], k == 0, k == NCH - 1, [("ring", sl2[cc]), ("xn", k)],
                       out_ap=ps[cc][:, 0:tw])
                    mm(2 + cc, wv[:, k, 128:256], xn[:, k, 0:tw], k == 0, k == NCH - 1, [("ring", sl2[cc]), ("xn", k)],
                       out_ap=ps[2 + cc][:, 0:tw])
            for c in range(NCH):
                ba, bg = c % 2, 2 + c % 2
                if c >= 2:
                    slot = load_unit("cin", c)
                    wv = ring[:, slot, :].rearrange("p (k c) -> p k c", k=NCH)
                    for k in range(NCH):
                        mm(ba, wv[:, k, 0:128], xn[:, k, 0:tw], k == 0, k == NCH - 1, [("ring", slot), ("xn", k)],
                           out_ap=ps[ba][:, 0:tw])
                    for k in range(NCH):
                        mm(bg, wv[:, k, 128:256], xn[:, k, 0:tw], k == 0, k == NCH - 1, [("ring", slot), ("xn", k)],
                           out_ap=ps[bg][:, 0:tw])
                flush_stat()
                fi = nxt("fb", NFB)
                act(fb[fi][:, 0:tw], ps[bg][:, 0:tw], AF.Sigmoid, [("ps", bg)], [("fb", fi), ("ps", bg)],
                    bias=col("b_in_g", c))
                ui = nxt("ub", NU)
                S.add("pool", (lambda e, ui=ui, c=c: e.tensor_copy(out=ub[ui][:, 0:CW - 1], in_=ucarry[:, c, :])),
                      reads=[("ucarry", c)], writes=[("ub", ui)])
                S.add("dve", (lambda e, ui=ui, c=c, fi=fi, ba=ba: e.scalar_tensor_tensor(
                    out=ub[ui][:, CW - 1:CW - 1 + tw], in0=ps[ba][:, 0:tw], scalar=col("b_in_a", c),
                    in1=fb[fi][:, 0:tw], op0=ALU.add, op1=ALU.mult)),
                    reads=[("ps", ba), ("fb", fi)] + CONST, writes=[("ub", ui), ("ps", ba)])
                if t == 0:
                    S.add("dve", (lambda e, ui=ui: e.tensor_tensor(out=ub[ui][:, CW - 1:CW - 1 + tw],
                                                                   in0=ub[ui][:, CW - 1:CW - 1 + tw],
                                                                   in1=mask0[:, 0:tw], op=ALU.mult)),
                          reads=[("ub", ui), "mask0"], writes=[("ub", ui)])
                S.add("pool", (lambda e, ui=ui, c=c: e.tensor_copy(out=ucarry[:, c, :], in_=ub[ui][:, tw:tw + CW - 1])),
                      reads=[("ub", ui)], writes=[("ucarry", c)])
                cur.append((c, ui))
                if len(cur) == 1 and prev is not None:
                    taps(prev)
                    prev = None
                if len(cur) == 2:
                    prev = cur
                    cur = []
            taps(prev)
            flush_stat()
            S.add("dve", lambda e: e.tensor_scalar(out=st1[:, 0:tw], in0=ps[6][:, 0:tw], scalar1=1.0 / D, scalar2=None,
                                                   op0=ALU.mult), reads=[("ps", 6)], writes=["st1", ("ps", 6)])
            S.add("dve", lambda e: e.tensor_tensor(out=st3[:, 0:tw], in0=st1[:, 0:tw], in1=st1[:, 0:tw], op=ALU.mult),
                  reads=["st1"], writes=["st3"])
            S.add("dve", lambda e: e.scalar_tensor_tensor(out=st3[:, 0:tw], in0=ps[7][:, 0:tw], scalar=1.0 / D,
                                                          in1=st3[:, 0:tw], op0=ALU.mult, op1=ALU.subtract),
                  reads=[("ps", 7), "st3"], writes=["st3", ("ps", 7)])
            act(st3[:, 0:tw], st3[:, 0:tw], AF.Ln, ["st3"], ["st3"], bias=eps_t["ln"])
            act(st2[:, 0:tw], st3[:, 0:tw], AF.Exp, ["st3"], ["st2"], scale=-0.5)
            for c in range(NCH):
                i = nxt("fb", NFB)
                S.add("dve", (lambda e, c=c, i=i: e.tensor_tensor(out=fb[i][:, 0:tw], in0=v[:, c, 0:tw],
                                                                  in1=st1[:, 0:tw], op=ALU.subtract)),
                      reads=[("v", c), "st1"], writes=[("fb", i)])
                S.add("dve", (lambda e, i=i: e.tensor_tensor(out=fb[i][:, 0:tw], in0=fb[i][:, 0:tw],
                                                             in1=st2[:, 0:tw], op=ALU.mult)),
                      reads=[("fb", i), "st2"], writes=[("fb", i)])
                act(xn[:, c, 0:tw], fb[i][:, 0:tw], AF.Silu, [("fb", i)], [("xn", c)],
                    bias=col("ln_b", c), scale=col("ln_g", c))
            proj_fm("cout", 0, NCH, lambda k: xn[:, k, 0:tw], lambda k: ("xn", k), tw,
                    lambda bank, m: evac_v_and_stat(bank, m, tw, col("b_out", m), m == 0, m == NCH - 1), kouter=True)
            post_norm_residual("g_mix_post0", tw)

        def ffn(l, tw):
            rmsnorm_to_xn("g_ffn_pre%d" % l, tw)
            for half in range(2):
                if half == 0:
                    sl2 = [load_unit("gu%d" % l, 0), load_unit("gu%d" % l, 1)]
                    for k in range(NCH):
                        for f in range(2):
                            wv = ring[:, sl2[f], :].rearrange("p (k c) -> p k c", k=NCH)
                            mm(f, wv[:, k, 0:128], xn[:, k, 0:tw], k == 0, k == NCH - 1, [("ring", sl2[f]), ("xn", k)],
                               out_ap=ps[f][:, 0:tw])
                            mm(2 + f, wv[:, k, 128:256], xn[:, k, 0:tw], k == 0, k == NCH - 1,
                               [("ring", sl2[f]), ("xn", k)], out_ap=ps[2 + f][:, 0:tw])
                for f in range(NFH):
                    bg, bu = f % 2, 2 + f % 2
                    if not (half == 0 and f < 2):
                        slot = load_unit("gu%d" % l, half * NFH + f)
                        wv = ring[:, slot, :].rearrange("p (k c) -> p k c", k=NCH)
                        for k in range(NCH):
                            mm(bg, wv[:, k, 0:128], xn[:, k, 0:tw], k == 0, k == NCH - 1, [("ring", slot), ("xn", k)],
                               out_ap=ps[bg][:, 0:tw])
                        for k in range(NCH):
                            mm(bu, wv[:, k, 128:256], xn[:, k, 0:tw], k == 0, k == NCH - 1, [("ring", slot), ("xn", k)],
                               out_ap=ps[bu][:, 0:tw])
                    flush_stat()
                    fi = nxt("fb", NFB)
                    act(fb[fi][:, 0:tw], ps[bg][:, 0:tw], AF.Silu, [("ps", bg)], [("fb", fi), ("ps", bg)])
                    S.add("dve", (lambda e, f=f, fi=fi, bu=bu: e.tensor_tensor(out=big[:, f, 0:tw], in0=ps[bu][:, 0:tw],
                                                                              in1=fb[fi][:, 0:tw], op=ALU.mult)),
                          reads=[("ps", bu), ("fb", fi)], writes=[("big", f), ("ps", bu)])
                for m in range(NCH):
                    slot = load_unit("dn%d" % l, half * NCH + m)
                    wv = ring[:, slot, 0:NFH * 128].rearrange("p (k c) -> p k c", k=NFH)
                    bank = 4 + m % 2
                    for k in range(NFH):
                        mm(bank, wv[:, k, :], big[:, k, 0:tw], k == 0, k == NFH - 1, [("ring", slot), ("big", k)],
                           out_ap=ps[bank][:, 0:tw])
                    if half == 0:
                        act(v[:, m, 0:tw], ps[bank][:, 0:tw], AF.Copy, [("ps", bank)], [("v", m), ("ps", bank)])
                    else:
                        S.add("dve", (lambda e, m=m, bank=bank: e.tensor_tensor(out=v[:, m, 0:tw], in0=ps[bank][:, 0:tw],
                                                                               in1=v[:, m, 0:tw], op=ALU.add)),
                              reads=[("ps", bank), ("v", m)], writes=[("v", m), ("ps", bank)])
                        i = nxt("sq", NSQ)
                        act(sqb[i][:, 0:tw], v[:, m, 0:tw], AF.Square, [("v", m)], [("sq", i)])
                        flush_stat()
                        pending_stat.append(stat_mm(7, i, tw, m == 0, m == NCH - 1, "sq"))
            post_norm_residual("g_ffn_post%d" % l, tw)

        def attn_kv(t, tw):
            def evac_k(bank, g):
                if t == 0:
                    act(kmeta[:, g, :], ps[bank][:, 0:NMETA], AF.Identity, [("ps", bank)], [("kmeta", g), ("ps", bank)],
                        bias=col("b_k", g))
                    act(kbuf[:, g, 3 * 128:4 * 128], ps[bank][:, T0 - 128:T0], AF.Identity, [("ps", bank)],
                        [("kbuf", g, 3), ("ps", bank)], bias=col("b_k", g))
                else:
                    s0 = 4 * (t % 2)
                    act(kbuf[:, g, s0 * 128:(s0 + 4) * 128], ps[bank][:, 0:TW], AF.Identity, [("ps", bank)],
                        [("kbuf", g, s0 + i) for i in range(4)] + [("ps", bank)], bias=col("b_k", g))
            proj_fm("qkv", 8, 4, lambda k: xn[:, k, 0:tw], lambda k: ("xn", k), tw, evac_k)
            slot = load_unit("qkv", 10)
            wv = ring[:, slot, :].rearrange("p (k c) -> p k c", k=NCH)
            if t == 0:
                blocks = [(T0 - 128, 128, 3, 0), (0, NMETA, None, 0), (0, NMETA, None, 32)]
            else:
                blocks = [(i * 128, 128, 4 * (t % 2) + i, 0) for i in range(4)]
            for bi, (o, n, vs, pr) in enumerate(blocks):
                bank = 4 + bi % 2
                for k in range(NCH):
                    mm(bank, xn[:, k, o:o + n], wv[:, k, :], k == 0, k == NCH - 1, [("ring", slot), ("xn", k)],
                       out_ap=ps[bank][pr:pr + n, 0:256])
                if vs is None:
                    S.add("dve", (lambda e, bank=bank, pr=pr: e.tensor_tensor(
                        out=vmeta[pr:pr + NMETA, :], in0=ps[bank][pr:pr + NMETA, 0:256],
                        in1=bv[pr:pr + NMETA, :], op=ALU.add)),
                        reads=[("ps", bank), "bv"], writes=["vmeta", ("ps", bank)])
                else:
                    S.add("dve", (lambda e, bank=bank, vs=vs: e.tensor_tensor(out=vbuf[:, vs, :], in0=ps[bank][:, 0:256],
                                                                             in1=bv[:, :], op=ALU.add)),
                          reads=[("ps", bank), "bv"], writes=[("vbuf", vs), ("ps", bank)])

        def attn_full(t):
            tw = TW
            rmsnorm_to_xn("g_mix_pre1", tw)

            def evac_q(bank, m):
                act(big[:, m, 0:tw], ps[bank][:, 0:tw], AF.Identity, [("ps", bank)], [("big", m), ("ps", bank)],
                    bias=col("bq_s", m), scale=0.125)
            proj_fm("qkv", 0, NCH, lambda k: xn[:, k, 0:tw], lambda k: ("xn", k), tw, evac_q, kouter=True)
            attn_kv(t, tw)
            steps = [(i, g, par) for i in range(4) for g in range(4) for par in range(2)]
            NS = len(steps)

            def emit_scores(n):
                i, g, par = steps[n]
                st = n % 2
                sc = 4 * (t % 2) + i
                sp_ = (sc - 1) % 8
                rows = slice(par * 64, par * 64 + 64)
                qs = slice(i * 128, (i + 1) * 128)
                qap = big[rows, 4 * g:4 * g + 4, qs]
                qres = [("big", 4 * g + j) for j in range(4)]
                mm(2 * st, kbuf[rows, g, sp_ * 128:(sp_ + 1) * 128], qap, True, False, [("kbuf", g, sp_)] + qres)
                mm(2 * st, ident[:], expb_p[:, par, 4 * g:4 * g + 4, :], False, True, ["ident", "expb_p"])
                mm(2 * st + 1, kbuf[rows, g, sc * 128:(sc + 1) * 128], qap, True, False, [("kbuf", g, sc)] + qres)
                mm(2 * st + 1, ident[:], expb_c[:, par, 4 * g:4 * g + 4, :], False, True, ["ident", "expb_c"])
                mr = 32 * st
                S.add("pe", lambda e: e.matmul(ps[4][mr:mr + NMETA, :], lhsT=kmeta[rows, g, :], rhs=qap, start=True, stop=True),
                      reads=[("kmeta", g)] + qres, writes=[("ps4", st)] + ([("ps", 4)] if n < 2 else []))

            def emit_softmax(n):
                i, g, par = steps[n]
                st = n % 2
                first = (t == 1 and i == 0)
                pis = []
                for kb in (0, 1):
                    bk = 2 * st + kb
                    pi = nxt("pb", NP)
                    act(pb[pi][:], ps[bk][:], AF.Exp, [("ps", bk)], [("pb", pi), ("ps", bk)])
                    if first and kb == 0:
                        S.add("dve", (lambda e, pi=pi: e.tensor_scalar(out=pb[pi][:], in0=pb[pi][:],
                                                                       scalar1=col("flag"), scalar2=None,
                                                                       op0=ALU.mult)),
                              reads=[("pb", pi)] + CONST, writes=[("pb", pi)])
                    pis.append(pi)
                mr = 32 * st
                mrows = slice(mr, mr + NMETA)
                ei = nxt("fb", NFB)
                act(fb[ei][mrows, :], ps[4][mrows, :], AF.Exp, [("ps4", st)],
                    [("fb", ei), ("ps4", st)] + ([("ps", 4)] if n >= NS - 2 else []))
                mi = nxt("pb", NP)
                if first:
                    xi = nxt("fb", NFB)
                    S.add("pool", (lambda e, xi=xi: e.dma_start(
                        out=fb[xi][mrows, :].rearrange("p (j q) -> p j q", j=4),
                        in_=mef_d[:, par, 4 * g:4 * g + 4, :])),
                        writes=[("fb", xi)], dma=("mef", xi))
                    act(fb[xi][mrows, :], fb[xi][mrows, :], AF.Exp, [("fb", xi)], [("fb", xi)])
                    S.add("dve", (lambda e, mi=mi, ei=ei, xi=xi: e.tensor_tensor(
                        out=pb[mi][mrows, :], in0=fb[ei][mrows, :], in1=fb[xi][mrows, :], op=ALU.mult)),
                        reads=[("fb", ei), ("fb", xi)], writes=[("pb", mi)])
                else:
                    S.add("dve", (lambda e, mi=mi, ei=ei: e.tensor_tensor(
                        out=pb[mi][mrows, :].rearrange("p (j q) -> p j q", j=4),
                        in0=fb[ei][mrows, :].rearrange("p (j q) -> p j q", j=4),
                        in1=mg[mrows, par * 16 + 4 * g:par * 16 + 4 * g + 4].unsqueeze(2).broadcast_to([NMETA, 4, 128]),
                        op=ALU.mult)),
                        reads=[("fb", ei), "mg"], writes=[("pb", mi)])
                return pis, mi

            def emit_pv(n, pis, mi):
                i, g, par = steps[n]
                st = n % 2
                sc = 4 * (t % 2) + i
                sp_ = (sc - 1) % 8
                rows = slice(par * 64, par * 64 + 64)
                qs = slice(i * 128, (i + 1) * 128)
                mrows = slice(32 * st, 32 * st + NMETA)
                vcols = slice(g * 64, (g + 1) * 64)
                nb = 5 + (i * 4 + g) % 2
                for bank, lt in ((nb, "v"), (7, "o")):
                    o_ap = ps[bank][rows, :]
                    l_p = vbuf[:, sp_, vcols] if lt == "v" else ones[:, 0:64]
                    l_c = vbuf[:, sc, vcols] if lt == "v" else ones[:, 0:64]
                    l_m = vmeta[mrows, vcols] if lt == "v" else ones[mrows, 0:64]
                    mm(bank, l_p, pb[pis[0]][:], True, False, [("vbuf", sp_), "ones", ("pb", pis[0])], out_ap=o_ap)
                    mm(bank, l_c, pb[pis[1]][:], False, False, [("vbuf", sc), "ones", ("pb", pis[1])], out_ap=o_ap)
                    mm(bank, l_m, pb[mi][mrows, :], False, True, ["vmeta", "ones", ("pb", mi)], out_ap=o_ap)
                if par == 1:
                    fi = nxt("fb", NFB)
                    S.add("dve", (lambda e, fi=fi: e.tensor_tensor(
                        out=fb[fi][:].rearrange("p (j q) -> p j q", j=4),
                        in0=ps[7][:].rearrange("p (j q) -> p j q", j=4),
                        in1=col("es", 4 * g, 4).unsqueeze(2).broadcast_to([128, 4, 128]), op=ALU.add)),
                        reads=[("ps", 7)] + CONST, writes=[("fb", fi), ("ps", 7)])
                    act(fb[fi][:], fb[fi][:], AF.Ln, [("fb", fi)], [("fb", fi)])
                    act(fb[fi][:], fb[fi][:], AF.Exp, [("fb", fi)], [("fb", fi)], scale=-1.0)
                    S.add("dve", (lambda e, fi=fi: e.tensor_tensor(
                        out=big[:, 4 * g:4 * g + 4, qs],
                        in0=ps[nb][:].rearrange("p (j q) -> p j q", j=4),
                        in1=fb[fi][:].rearrange("p (j q) -> p j q", j=4), op=ALU.mult)),
                        reads=[("ps", nb), ("fb", fi)], writes=[("big", 4 * g + j) for j in range(4)] + [("ps", nb)])

            emit_scores(0)
            for n in range(NS):
                if n + 1 < NS:
                    emit_scores(n + 1)
                pis, mi = emit_softmax(n)
                emit_pv(n, pis, mi)
            proj_fm("wo", 0, NCH, lambda k: big[:, k, 0:tw], lambda k: ("big", k), tw,
                    lambda bank, m: evac_v_and_stat(bank, m, tw, col("b_o", m), m == 0, m == NCH - 1))
            post_norm_residual("g_mix_post1", tw)

        allh = [("h", c) for c in range(NCH)]
        for t in range(NT + 1):
            S.seg = t + 1
            tw = T0 if t == 0 else TW
            off = 0 if t == 0 else T0 + (t - 1) * TW
            for c4 in range(0, NCH, 4):
                S.add("sp", (lambda e, off=off, tw=tw, c4=c4: e.dma_start(out=h[:, c4:c4 + 4, 0:tw],
                                                                          in_=xin[:, c4:c4 + 4, off:off + tw])),
                      writes=[("h", c) for c in range(c4, c4 + 4)], dma=("xin", c4))
            conv_mixer(t, tw)
            ffn(0, tw)
            if t == 0:
                emit_casts(LATE, extra_reads=[("h", 0)])
                rmsnorm_to_xn("g_mix_pre1", tw)
                attn_kv(0, tw)
            else:
                attn_full(t)
                ffn(1, tw)
                for c4 in range(0, NCH, 4):
                    S.add("pool", (lambda e, t=t, c4=c4: e.dma_start(out=out_d[:, c4:c4 + 4, (t - 1) * TW:t * TW],
                                                                     in_=h[:, c4:c4 + 4, :])),
                          reads=[("h", c) for c in range(c4, c4 + 4)], dma=("out", c4))
        S.add("sp", lambda e: None, reads=allh + ["outdone"], writes=allh)
        S.emit(nc, es)
    return nc


def _t5_bucket(dist):
    dist = np.asarray(dist, np.int64)
    d = np.maximum(dist, 16).astype(np.float32)
    large = 16 + (np.log(d / np.float32(16.0)) / np.float32(math.log(8.0)) * np.float32(16.0)).astype(np.int32)
    return np.where(dist < 16, dist, np.minimum(large, 31)).astype(np.int64)


def _vec_cols(vv):
    return np.ascontiguousarray(np.asarray(vv, np.float32).reshape(-1, 128).T)


def _img(w, ncols):
    k = w.shape[0] // 128
    return np.ascontiguousarray(w.reshape(k, 128, ncols).transpose(1, 0, 2)).reshape(128, k * ncols)


def _prep_shared(inp):
    f = lambda k: np.asarray(inp[k], np.float32)
    sh = {}
    w_in = f("conv_w_in")[0]
    sh["w_cin"] = np.stack([_img(np.concatenate([w_in[:, c * 128:(c + 1) * 128],
                                                 w_in[:, D + c * 128:D + (c + 1) * 128]], 1), 256) for c in range(16)])
    wdw_ = f("conv_w_dw")[0]
    dg = np.zeros((16, 128, NPE, 128), np.float32)
    pp = np.arange(128)
    for c in range(16):
        for jj in range(NPE):
            dg[c, pp, jj, pp] = wdw_[CW - NPE + jj, c * 128 + pp]
    sh["w_dg"] = dg.reshape(16, 128, NPE * 128)
    w_out = f("conv_w_out")[0]
    sh["w_cout"] = np.stack([_img(w_out[:, u * 256:(u + 1) * 256], 256) for u in range(8)])
    for l in range(2):
        wg, wu, wd = f("ffn_w_gate")[l], f("ffn_w_up")[l], f("ffn_w_down")[l]
        sh["w_gu%d" % l] = np.stack([_img(np.concatenate([wg[:, i * 128:(i + 1) * 128],
                                                          wu[:, i * 128:(i + 1) * 128]], 1), 256) for i in range(NF)])
        sh["w_dn%d" % l] = np.stack([_img(wd[half * NFH * 128:(half + 1) * NFH * 128, m * 128:(m + 1) * 128], 128)
                                     for half in range(2) for m in range(16)])
    wqkv = f("attn_w_qkv")[0]
    units = [_img(wqkv[:, u * 256:(u + 1) * 256], 256) for u in range(8)]
    wk = wqkv[:, D:D + 256]
    wv = wqkv[:, D + 256:D + 512]
    for u in range(2):
        kv0, kv1 = 2 * u, 2 * u + 1
        units.append(_img(np.concatenate([wk[:, kv0 * 64:(kv0 + 1) * 64]] * 2 + [wk[:, kv1 * 64:(kv1 + 1) * 64]] * 2, 1), 256))
    units.append(_img(wv, 256))
    sh["w_qkv"] = np.stack(units)
    wo = f("attn_w_o")[0]
    sh["w_wo"] = np.stack([_img(wo[:, u * 256:(u + 1) * 256], 256) for u in range(8)])

    cvt = np.zeros((128, NCV), np.float32)

    def put(name, arr):
        arr = np.asarray(arr, np.float32)
        cvt[:, _cols[name]:_cols[name] + arr.shape[1]] = arr
    for l in range(2):
        put("g_mix_pre%d" % l, _vec_cols(f("norm_mix_pre")[l]))
        put("g_mix_post%d" % l, _vec_cols(f("norm_mix_post")[l]))
        put("g_ffn_pre%d" % l, _vec_cols(f("norm_ffn_pre")[l]))
        put("g_ffn_post%d" % l, _vec_cols(f("norm_ffn_post")[l]))
    b_in = f("conv_b_in")[0]
    put("b_in_a", _vec_cols(b_in[:D]))
    put("b_in_g", _vec_cols(b_in[D:]))
    put("b_dw", _vec_cols(f("conv_b_dw")[0]))
    put("ln_g", _vec_cols(f("conv_ln_g")[0]))
    put("ln_b", _vec_cols(f("conv_ln_b")[0]))
    put("b_out", _vec_cols(f("conv_b_out")[0]))
    bqkv = f("attn_b_qkv")[0]
    put("b_q", _vec_cols(bqkv[:D]))
    put("b_o", _vec_cols(f("attn_b_o")[0]))
    bk = bqkv[D:D + 256].reshape(4, 64)
    put("b_k", np.concatenate([bk, bk], 1).T)
    sinks = f("attn_sinks")[0]
    put("sinks", np.stack([sinks[2 * np.arange(16) + (1 if p >= 64 else 0)] for p in range(128)]))
    wdw = f("conv_w_dw")[0]
    put("w_dw", np.concatenate([_vec_cols(wdw[j]) for j in range(CW)], 1))
    sh["cv"] = cvt
    sh["bv_rep"] = np.ascontiguousarray(np.broadcast_to(bqkv[D + 256:D + 512][None, :], (128, 256)))
    rb = f("rel_bias")
    sh["mg"] = np.ascontiguousarray(np.broadcast_to(
        np.concatenate([rb[31, 0::2], rb[31, 1::2]])[None, :], (16, 32)))
    kk = np.arange(128)[:, None]
    qq = np.arange(128)[None, :]
    hd = (2 * np.arange(16)[None, :] + np.arange(2)[:, None])

    def table(dist, valid):
        g = rb[_t5_bucket(np.maximum(dist, 0))]
        tab = g[:, :, hd]
        tab = np.where(valid[:, :, None, None], tab, np.float32(NEG))
        return np.ascontiguousarray(tab.transpose(0, 2, 3, 1)).reshape(dist.shape[0], -1).astype(np.float32)
    sh["ident"] = np.eye(128, dtype=np.float32)
    sh["bias_prev"] = table(128 + qq - kk, kk > qq)
    sh["bias_cur"] = table(qq - kk, kk <= qq)
    mm_ = np.arange(NMETA)[:, None]
    sh["_mef_first"] = table(NMETA + qq - mm_, np.ones((NMETA, 128), bool)).reshape(NMETA, 2, 16, 128)
    sh["_mef_gen"] = table(np.full((NMETA, 128), 1000), np.ones((NMETA, 128), bool)).reshape(NMETA, 2, 16, 128)
    return sh


def _run(inp, cores_per_batch, nt):
    x = np.asarray(inp["x"], np.float32)
    B, S_, _ = x.shape
    assert S_ == cores_per_batch * nt * TW
    meta = np.asarray(inp["meta_tokens"], np.float32)
    sh = _prep_shared(inp)
    mef_first = sh.pop("_mef_first")
    mef_gen = sh.pop("_mef_gen")
    in_maps = []
    for b in range(B):
        for q in range(cores_per_batch):
            s0 = q * nt * TW
            if q == 0:
                halo = np.concatenate([np.zeros((HALO - NMETA, D), np.float32), meta], 0)
            else:
                halo = x[b, s0 - HALO:s0]
            stream = np.concatenate([meta, halo, x[b, s0:s0 + nt * TW]], 0)
            ntok = stream.shape[0]
            xin = np.ascontiguousarray(stream.T.reshape(16, 128, ntok).transpose(1, 0, 2))
            mask0 = np.ones((128, T0), np.float32)
            cvt = sh["cv"].copy()
            if q == 0:
                mask0[:, NMETA:T0 - NMETA] = 0.0
                cvt[:, _cols["flag"]] = 0.0
            else:
                cvt[:, _cols["flag"]] = 1.0
            m = dict(sh)
            m["cv"] = cvt
            m["xin"] = xin
            m["mask0"] = mask0
            m["mef"] = np.ascontiguousarray(mef_first if q == 0 else mef_gen)
            in_maps.append(m)
    nc = build_program(nt)
    res = run_bass_kernel_spmd(nc, in_maps, core_ids=list(range(len(in_maps))))
    out = np.empty((B, S_, D), np.float32)
    i = 0
    for b in range(B):
        for q in range(cores_per_batch):
            o = np.asarray(res.results[i]["out"])
            out[b, q * nt * TW:(q + 1) * nt * TW] = o.transpose(2, 1, 0).reshape(nt * TW, D)
            i += 1
    return out


def kernel(**inputs):
    return _run(inputs, cores_per_batch=4, nt=8)
```

```python
import math
from contextlib import ExitStack

import numpy as np
import concourse.bass as bass
import concourse.mybir as mybir
from concourse.bass_utils import run_bass_kernel_spmd

F32 = mybir.dt.float32
BF16 = mybir.dt.bfloat16
AF = mybir.ActivationFunctionType
ALU = mybir.AluOpType

D = 2048
NCH = 16
DFF = 5632
NF = 44
NFH = 22
NMETA = 16
HALO = 176
T0 = NMETA + HALO
TW = 512
CW = 31
NHEADS = 32
RMS_EPS = 1e-6
LN_EPS = 1e-5
NEG = -30000.0
NSLOT = 4
SLOT_EL = 4096

_cols = {}
_n = 0
for _name, _w in [("g_mix_pre0", 16), ("g_mix_post0", 16), ("g_ffn_pre0", 16), ("g_ffn_post0", 16),
                  ("g_mix_pre1", 16), ("g_mix_post1", 16), ("g_ffn_pre1", 16), ("g_ffn_post1", 16),
                  ("b_in_a", 16), ("b_in_g", 16), ("b_dw", 16), ("ln_g", 16), ("ln_b", 16),
                  ("b_out", 16), ("b_q", 16), ("b_o", 16), ("b_k", 4), ("sinks", 16),
                  ("w_dw", CW * 16), ("flag", 1), ("bq_s", 16), ("es", 16)]:
    _cols[_name] = _n
    _n += _w
NCV = _n


class Op:
    __slots__ = ("eng", "fn", "deps", "signal", "sigval", "seg", "pos", "is_dma", "dkey", "dval", "idx")


class Sched:
    ENGS = ("pe", "act", "dve", "pool", "sp")

    def __init__(self):
        self.ops = []
        self.by_eng = {e: [] for e in self.ENGS}
        self.last_w = {}
        self.readers = {}
        self.seg = 0
        self.dma_count = {}

    def add(self, eng, fn, reads=(), writes=(), dma=None, nodep_writes=()):
        op = Op()
        op.eng = eng
        op.fn = fn
        op.signal = False
        op.sigval = 0
        op.seg = self.seg
        op.is_dma = dma is not None
        op.dkey = dma
        op.idx = len(self.ops)
        deps = {}
        for r in reads:
            w = self.last_w.get(r)
            if w is not None:
                deps[w.idx] = w
        for r in writes:
            w = self.last_w.get(r)
            if w is not None:
                deps[w.idx] = w
            for o in self.readers.get(r, {}).values():
                deps[o.idx] = o
        for r in reads:
            self.readers.setdefault(r, {})[eng if not op.is_dma else ("dma", op.idx)] = op
        for r in list(writes) + list(nodep_writes):
            self.last_w[r] = op
            self.readers[r] = {}
        deps.pop(op.idx, None)
        op.deps = list(deps.values())
        if op.is_dma:
            self.dma_count[dma] = self.dma_count.get(dma, 0) + 1
            op.dval = 16 * self.dma_count[dma]
        op.pos = len(self.by_eng[eng])
        self.by_eng[eng].append(op)
        self.ops.append(op)
        return op

    def _needs_wait(self, op, d):
        if d.is_dma:
            return True
        if d.eng == op.eng:
            if op.is_dma:
                return True
            if op.eng == "pe":
                return False
            return (op.pos - d.pos) <= 1
        return True

    def emit(self, nc, es):
        for op in self.ops:
            for d in op.deps:
                if not d.is_dma and self._needs_wait(op, d):
                    d.signal = True
        cnt = {}
        for op in self.ops:
            if op.signal and not op.is_dma:
                k = (op.eng, op.seg)
                cnt[k] = cnt.get(k, 0) + 1
                op.sigval = cnt[k]
        esem = {k: es.enter_context(nc.semaphore("s_%s_%d" % k)) for k in cnt}
        dsem = {}
        for i, k in enumerate(self.dma_count):
            dsem[k] = es.enter_context(nc.semaphore("d_%d" % i))
        block = es.enter_context(nc.Block())

        def run(e, name):
            waited = {}
            for op in self.by_eng[name]:
                for d in op.deps:
                    if not self._needs_wait(op, d):
                        continue
                    if d.is_dma:
                        sem, val = dsem[d.dkey], d.dval
                    else:
                        sem, val = esem[(d.eng, d.seg)], d.sigval
                    key = id(sem)
                    if waited.get(key, 0) >= val:
                        continue
                    e.wait_ge(sem, val)
                    waited[key] = val
                ins = op.fn(e)
                if ins is None:
                    continue
                if op.is_dma:
                    ins.then_inc(dsem[op.dkey], 16)
                elif op.signal:
                    ins.then_inc(esem[(op.eng, op.seg)], 1)

        @block.tensor
        def _(e):
            run(e, "pe")

        @block.scalar
        def _(e):
            run(e, "act")

        @block.vector
        def _(e):
            run(e, "dve")

        @block.gpsimd
        def _(e):
            run(e, "pool")

        @block.sync
        def _(e):
            run(e, "sp")


def build_program(NT):
    NTOK = T0 + NT * TW
    nc = bass.Bass("TRN2", target_bir_lowering=False)
    S = Sched()

    def din(name, shape, dt=F32):
        return nc.dram_tensor(name, list(shape), dt, kind="ExternalInput").ap()

    xin = din("xin", [128, NCH, NTOK])
    cv_d = din("cv", [128, NCV])
    bv_d = din("bv_rep", [128, 256])
    mask_d = din("mask0", [128, T0])
    mg_d = din("mg", [16, 32])
    bprev_d = din("bias_prev", [128, 4096])
    bcur_d = din("bias_cur", [128, 4096])
    ident_d = din("ident", [128, 128])
    mef_d = din("mef", [16, 2, NCH, 128])
    out_d = nc.dram_tensor("out", [128, NCH, NT * TW], F32, kind="ExternalOutput").ap()

    streams = {"cin": (16, 4096), "dg": (16, 4096), "cout": (8, 4096), "gu0": (NF, 4096), "dn0": (32, 2816),
               "qkv": (11, 4096), "wo": (8, 4096), "gu1": (NF, 4096), "dn1": (32, 2816)}
    w32 = {}
    wbf = {}
    for k, (nu, x) in streams.items():
        w32[k] = din("w_" + k, [nu, 128, x])
        wbf[k] = nc.dram_tensor("wb_" + k, [nu, 128, x], BF16, kind="Internal").ap()

    with ExitStack() as es:
        def sb(name, shape, dt):
            return es.enter_context(nc.sbuf_tensor(name, list(shape), dt))

        h = sb("h", [128, NCH, TW], F32)
        xn = sb("xn", [128, NCH, TW], BF16)
        v = sb("v", [128, NCH, TW], F32)
        big = sb("big", [128, NFH, TW], BF16)
        ring = sb("ring", [128, NSLOT, SLOT_EL], BF16)
        kbuf = sb("kbuf", [128, 4, 8 * 128], BF16)
        vbuf = sb("vbuf", [128, 8, 256], BF16)
        kmeta = sb("kmeta", [128, 4, NMETA], BF16)
        vmeta = sb("vmeta", [48, 256], BF16)
        expb_p = sb("expb_p", [128, 2, NCH, 128], BF16)
        expb_c = sb("expb_c", [128, 2, NCH, 128], BF16)
        ident = sb("ident_t", [128, 128], BF16)
        cv = sb("cvt", [128, NCV], F32)
        bv = sb("bvt", [128, 256], F32)
        mask0 = sb("mask0t", [128, T0], F32)
        mg = sb("mgt", [48, 32], F32)
        ones = sb("ones", [128, 128], BF16)
        ucarry = sb("ucarry", [128, NCH, CW - 1], BF16)
        NU = 4
        ub = [sb("ub%d" % i, [128, CW - 1 + TW], BF16) for i in range(NU)]
        NSQ = 4
        sqb = [sb("sqb%d" % i, [128, TW], BF16) for i in range(NSQ)]
        ybfb = [sb("ybf%d" % i, [128, TW], BF16) for i in range(2)]
        NFB = 4
        fb = [sb("fb%d" % i, [128, TW], F32) for i in range(NFB)]
        st1 = sb("st1", [128, TW], F32)
        st2 = sb("st2", [128, TW], F32)
        st3 = sb("st3", [128, TW], F32)
        NP = 8
        pb = [sb("pb%d" % i, [128, TW], BF16) for i in range(NP)]
        ps = [es.enter_context(nc.psum_tensor("ps%d" % i, [128, TW], F32)) for i in range(8)]

        def col(name, c=0, n=1):
            o = _cols[name] + c
            return cv[:, o:o + n]

        cnt = {"u": 0, "sq": 0, "fb": 0, "pb": 0, "pm": 0, "ybf": 0, "ub": 0}

        def nxt(k, n):
            i = cnt[k] % n
            cnt[k] += 1
            return i

        S.seg = 0
        S.add("sp", lambda e: e.dma_start(out=cv[:], in_=cv_d), writes=["cv"], dma="c_cv")
        S.add("sp", lambda e: e.dma_start(out=bv[:], in_=bv_d), writes=["bv"], dma="c_bv")
        S.add("sp", lambda e: e.dma_start(out=mask0[:], in_=mask_d), writes=["mask0"], dma="c_mask")
        S.add("sp", lambda e: e.dma_start(out=mg[0:16, :], in_=mg_d), nodep_writes=["mg"], dma="c_mg")
        S.add("sp", lambda e: e.dma_start(out=mg[32:48, :], in_=mg_d), nodep_writes=["mg"], dma="c_mg")
        ebp = expb_p[:].rearrange("p a c q -> p (a c q)")
        ebc = expb_c[:].rearrange("p a c q -> p (a c q)")
        S.add("pool", lambda e: e.dma_start(out=ebp, in_=bprev_d), writes=["expb_p"], dma="c_bp")
        S.add("pool", lambda e: e.dma_start(out=ebc, in_=bcur_d), writes=["expb_c"], dma="c_bc")
        S.add("pool", lambda e: e.dma_start(out=ident[:], in_=ident_d), writes=["ident"], dma="c_id")
        LATE = ("wo", "gu1", "dn1")
        FINE = ("cin",)

        def wres(k, u):
            if k in FINE:
                return (k, u)
            if k == "qkv":
                return (k, "q") if u < 8 else (k, "kv")
            return (k,)

        def emit_casts(keys, extra_reads=(), sel=None):
            for k in keys:
                nu = streams[k][0]
                for u in range(nu):
                    if sel is not None and not sel(k, u):
                        continue
                    r = wres(k, u)
                    S.add("pool", (lambda e, k=k, u=u: e.dma_start(out=wbf[k][u], in_=w32[k][u])),
                          reads=list(extra_reads), nodep_writes=[("wbf",) + r], dma=("cast",) + r)

        emit_casts([k for k in streams if k not in LATE], sel=lambda k, u: not (k == "qkv" and u < 8))
        S.add("dve", lambda e: e.memset(ones[:], 1.0), writes=["ones"])
        S.add("dve", lambda e: e.memset(ucarry[:], 0.0), writes=[("ucarry", c) for c in range(NCH)])
        S.add("dve", lambda e: e.tensor_scalar(out=col("bq_s", 0, 16), in0=col("b_q", 0, 16), scalar1=0.125,
                                               scalar2=None, op0=ALU.mult), reads=["cv"], writes=["cv2"])
        S.add("act", lambda e: e.activation(out=col("es", 0, 16), in_=col("sinks", 0, 16), func=AF.Exp),
              reads=["cv"], writes=["cv3"])
        S.add("act", lambda e: e.activation(out=mg[0:16, :], in_=mg[0:16, :], func=AF.Exp), reads=["mg"], writes=["mg"])
        S.add("act", lambda e: e.activation(out=mg[32:48, :], in_=mg[32:48, :], func=AF.Exp), reads=["mg"], writes=["mg"])
        CONST = ["cv", "cv2", "cv3"]

        unit_ctr = [0]

        def load_unit(stream, u):
            slot = unit_ctr[0] % NSLOT
            unit_ctr[0] += 1
            x = streams[stream][1]
            S.add("sp", lambda e: e.dma_start(out=ring[:, slot, 0:x], in_=wbf[stream][u]),
                  reads=[("wbf",) + wres(stream, u)], writes=[("ring", slot)],
                  dma=("ring", slot))
            return slot

        def mm(bank, lhsT, rhs, start, stop, reads, out_ap=None):
            o = ps[bank][:] if out_ap is None else out_ap
            S.add("pe", lambda e: e.matmul(o, lhsT=lhsT, rhs=rhs, start=start, stop=stop),
                  reads=reads, writes=[("ps", bank)])

        def act(out, in_, func, reads, writes, bias=None, scale=None):
            kw = {}
            if bias is not None:
                kw["bias"] = bias
            if scale is not None:
                kw["scale"] = scale
            S.add("act", lambda e: e.activation(out=out, in_=in_, func=func, **kw),
                  reads=list(reads) + CONST, writes=writes)

        pending_stat = []

        def flush_stat():
            while pending_stat:
                pending_stat.pop(0)()

        def stat_mm(bank, sqi, tw, first, last, src_key):
            def f():
                mm(bank, ones[:], sqb[sqi][:, 0:tw] if src_key == "sq" else ybfb[sqi][:, 0:tw], first, last,
                   ["ones", (src_key, sqi)], out_ap=ps[bank][:, 0:tw])
            return f

        def rstd_from(bank, tw, eps):
            act(st1[:, 0:tw], ps[bank][:, 0:tw], AF.Ln, [("ps", bank)], ["st1", ("ps", bank)],
                bias=eps_t[eps], scale=1.0 / D)
            act(st2[:, 0:tw], st1[:, 0:tw], AF.Exp, ["st1"], ["st2"], scale=-0.5)

        eps_t = {}
        for nm, val in (("rms", RMS_EPS), ("ln", LN_EPS)):
            t_ = sb("eps_" + nm, [128, 1], F32)
            S.add("dve", (lambda e, t_=t_, val=val: e.memset(t_[:], val)), writes=["eps_" + nm])
            eps_t[nm] = t_[:]
            CONST.append("eps_" + nm)

        def rmsnorm_to_xn(gname, tw, split=False):
            for c in range(NCH):
                i = nxt("sq", NSQ)
                if c % 2 == 0 or not split:
                    act(sqb[i][:, 0:tw], h[:, c, 0:tw], AF.Square, [("h", c)], [("sq", i)])
                else:
                    S.add("dve", (lambda e, i=i, c=c: e.tensor_tensor(out=sqb[i][:, 0:tw], in0=h[:, c, 0:tw],
                                                                      in1=h[:, c, 0:tw], op=ALU.mult)),
                          reads=[("h", c)], writes=[("sq", i)])
                mm(7, ones[:], sqb[i][:, 0:tw], c == 0, c == NCH - 1, ["ones", ("sq", i)], out_ap=ps[7][:, 0:tw])
            rstd_from(7, tw, "rms")
            for c in range(NCH):
                S.add("dve", (lambda e, c=c: e.scalar_tensor_tensor(out=xn[:, c, 0:tw], in0=h[:, c, 0:tw],
                                                                    scalar=col(gname, c), in1=st2[:, 0:tw],
                                                                    op0=ALU.mult, op1=ALU.mult)),
                      reads=[("h", c), "st2"] + CONST, writes=[("xn", c)])

        def post_norm_residual(gname, tw):
            flush_stat()
            rstd_from(7, tw, "rms")
            for c in range(NCH):
                i = nxt("fb", NFB)
                S.add("dve", (lambda e, c=c, i=i: e.scalar_tensor_tensor(out=fb[i][:, 0:tw], in0=v[:, c, 0:tw],
                                                                         scalar=col(gname, c), in1=st2[:, 0:tw],
                                                                         op0=ALU.mult, op1=ALU.mult)),
                      reads=[("v", c), "st2"] + CONST, writes=[("fb", i)])
                S.add("dve", (lambda e, c=c, i=i: e.tensor_tensor(out=h[:, c, 0:tw], in0=h[:, c, 0:tw],
                                                                  in1=fb[i][:, 0:tw], op=ALU.add)),
                      reads=[("h", c), ("fb", i)], writes=[("h", c)])

        def evac_v_and_stat(bank, m, tw, bias_ap, first, last):
            i = nxt("sq", NSQ)
            if bias_ap is not None:
                act(sqb[i][:, 0:tw], ps[bank][:, 0:tw], AF.Square, [("ps", bank)], [("sq", i), ("ps", bank)], bias=bias_ap)
                act(v[:, m, 0:tw], ps[bank][:, 0:tw], AF.Identity, [("ps", bank)], [("v", m), ("ps", bank)], bias=bias_ap)
            else:
                act(sqb[i][:, 0:tw], ps[bank][:, 0:tw], AF.Square, [("ps", bank)], [("sq", i), ("ps", bank)])
                act(v[:, m, 0:tw], ps[bank][:, 0:tw], AF.Copy, [("ps", bank)], [("v", m), ("ps", bank)])
            flush_stat()
            pending_stat.append(stat_mm(7, i, tw, first, last, "sq"))

        def proj_fm(stream, u0, n_out, rhs_of, rhs_res, tw, evac, cols_per_unit=2, kouter=False):
            slot = None
            m0 = 0
            if kouter and n_out >= 4:
                slots = [load_unit(stream, u0), load_unit(stream, u0 + 1)]
                for k in range(NCH):
                    for m in range(4):
                        sl = slots[m // 2]
                        wv = ring[:, sl, :].rearrange("p (k c) -> p k c", k=NCH)
                        lo = (m % 2) * 128
                        mm(m, wv[:, k, lo:lo + 128], rhs_of(k), k == 0, k == NCH - 1,
                           [("ring", sl), rhs_res(k)], out_ap=ps[m][:, 0:tw])
                for m in range(4):
                    evac(m, m)
                m0 = 4
            for m in range(m0, n_out):
                if m % cols_per_unit == 0:
                    slot = load_unit(stream, u0 + m // cols_per_unit)
                wv = ring[:, slot, :].rearrange("p (k c) -> p k c", k=NCH)
                lo = (m % cols_per_unit) * 128
                bank = m % 4
                for k in range(NCH):
                    mm(bank, wv[:, k, lo:lo + 128], rhs_of(k), k == 0, k == NCH - 1,
                       [("ring", slot), rhs_res(k)], out_ap=ps[bank][:, 0:tw])
                evac(bank, m)

        def conv_mixer(t, tw):
            rmsnorm_to_xn("g_mix_pre0", tw, split=True)

            def taps(c, ui):
                slot = load_unit("dg", c)
                dv = ring[:, slot, :].rearrange("p (j m) -> p j m", j=32)
                bank = 4 + c % 2
                for j in range(CW):
                    mm(bank, dv[:, j, :], ub[ui][:, j:j + tw], j == 0, j == CW - 1, [("ring", slot), ("ub", ui)],
                       out_ap=ps[bank][:, 0:tw])
                act(v[:, c, 0:tw], ps[bank][:, 0:tw], AF.Identity, [("ps", bank)], [("v", c), ("ps", bank)],
                    bias=col("b_dw", c))
                yi = nxt("ybf", 2)
                act(ybfb[yi][:, 0:tw], v[:, c, 0:tw], AF.Copy, [("v", c)], [("ybf", yi)])
                si = nxt("sq", NSQ)
                act(sqb[si][:, 0:tw], v[:, c, 0:tw], AF.Square, [("v", c)], [("sq", si)])
                pending_stat.append(stat_mm(6, yi, tw, c == 0, c == NCH - 1, "ybf"))
                pending_stat.append(stat_mm(7, si, tw, c == 0, c == NCH - 1, "sq"))

            prev = None
            sl2 = [load_unit("cin", 0), load_unit("cin", 1)]
            for k in range(NCH):
                for cc in range(2):
                    wv = ring[:, sl2[cc], :].rearrange("p (k c) -> p k c", k=NCH)
                    mm(cc, wv[:, k, 0:128], xn[:, k, 0:tw], k == 0, k == NCH - 1, [("ring", sl2[cc]), ("xn", k)],
                       out_ap=ps[cc][:, 0:tw])
                    mm(2 + cc, wv[:, k, 128:256], xn[:, k, 0:tw], k == 0, k == NCH - 1, [("ring", sl2[cc]), ("xn", k)],
                       out_ap=ps[2 + cc][:, 0:tw])
            for c in range(NCH):
                ba, bg = c % 2, 2 + c % 2
                if c >= 2:
                    slot = load_unit("cin", c)
                    wv = ring[:, slot, :].rearrange("p (k c) -> p k c", k=NCH)
                    for k in range(NCH):
                        mm(ba, wv[:, k, 0:128], xn[:, k, 0:tw], k == 0, k == NCH - 1, [("ring", slot), ("xn", k)],
                           out_ap=ps[ba][:, 0:tw])
                    for k in range(NCH):
                        mm(bg, wv[:, k, 128:256], xn[:, k, 0:tw], k == 0, k == NCH - 1, [("ring", slot), ("xn", k)],
                           out_ap=ps[bg][:, 0:tw])
                flush_stat()
                ui = nxt("ub", NU)
                S.add("act", (lambda e, ui=ui, c=c: e.activation(out=ub[ui][:, 0:CW - 1], in_=ucarry[:, c, :], func=AF.Copy)),
                      reads=[("ucarry", c)], writes=[("ub", ui)])
                fi = nxt("fb", NFB)
                act(fb[fi][:, 0:tw], ps[bg][:, 0:tw], AF.Sigmoid, [("ps", bg)], [("fb", fi), ("ps", bg)],
                    bias=col("b_in_g", c))
                S.add("dve", (lambda e, ui=ui, c=c, fi=fi, ba=ba: e.scalar_tensor_tensor(
                    out=ub[ui][:, CW - 1:CW - 1 + tw], in0=ps[ba][:, 0:tw], scalar=col("b_in_a", c),
                    in1=fb[fi][:, 0:tw], op0=ALU.add, op1=ALU.mult)),
                    reads=[("ps", ba), ("fb", fi)] + CONST, writes=[("ub", ui), ("ps", ba)])
                if t == 0:
                    S.add("dve", (lambda e, ui=ui: e.tensor_tensor(out=ub[ui][:, CW - 1:CW - 1 + tw],
                                                                   in0=ub[ui][:, CW - 1:CW - 1 + tw],
                                                                   in1=mask0[:, 0:tw], op=ALU.mult)),
                          reads=[("ub", ui), "mask0"], writes=[("ub", ui)])
                if prev is not None:
                    taps(*prev)
                prev = (c, ui)
                S.add("act", (lambda e, ui=ui, c=c: e.activation(out=ucarry[:, c, :], in_=ub[ui][:, tw:tw + CW - 1],
                                                                 func=AF.Copy)),
                      reads=[("ub", ui)], writes=[("ucarry", c)])
            taps(*prev)
            flush_stat()
            S.add("dve", lambda e: e.tensor_scalar(out=st1[:, 0:tw], in0=ps[6][:, 0:tw], scalar1=1.0 / D, scalar2=None,
                                                   op0=ALU.mult), reads=[("ps", 6)], writes=["st1", ("ps", 6)])
            S.add("dve", lambda e: e.tensor_tensor(out=st3[:, 0:tw], in0=st1[:, 0:tw], in1=st1[:, 0:tw], op=ALU.mult),
                  reads=["st1"], writes=["st3"])
            S.add("dve", lambda e: e.scalar_tensor_tensor(out=st3[:, 0:tw], in0=ps[7][:, 0:tw], scalar=1.0 / D,
                                                          in1=st3[:, 0:tw], op0=ALU.mult, op1=ALU.subtract),
                  reads=[("ps", 7), "st3"], writes=["st3", ("ps", 7)])
            act(st3[:, 0:tw], st3[:, 0:tw], AF.Ln, ["st3"], ["st3"], bias=eps_t["ln"])
            act(st2[:, 0:tw], st3[:, 0:tw], AF.Exp, ["st3"], ["st2"], scale=-0.5)
            for c in range(NCH):
                i = nxt("fb", NFB)
                S.add("dve", (lambda e, c=c, i=i: e.tensor_tensor(out=fb[i][:, 0:tw], in0=v[:, c, 0:tw],
                                                                  in1=st1[:, 0:tw], op=ALU.subtract)),
                      reads=[("v", c), "st1"], writes=[("fb", i)])
                S.add("dve", (lambda e, i=i: e.tensor_tensor(out=fb[i][:, 0:tw], in0=fb[i][:, 0:tw],
                                                             in1=st2[:, 0:tw], op=ALU.mult)),
                      reads=[("fb", i), "st2"], writes=[("fb", i)])
                act(xn[:, c, 0:tw], fb[i][:, 0:tw], AF.Silu, [("fb", i)], [("xn", c)],
                    bias=col("ln_b", c), scale=col("ln_g", c))
            proj_fm("cout", 0, NCH, lambda k: xn[:, k, 0:tw], lambda k: ("xn", k), tw,
                    lambda bank, m: evac_v_and_stat(bank, m, tw, col("b_out", m), m == 0, m == NCH - 1), kouter=True)
            post_norm_residual("g_mix_post0", tw)

        def ffn(l, tw):
            rmsnorm_to_xn("g_ffn_pre%d" % l, tw)
            for half in range(2):
                if half == 0:
                    sl2 = [load_unit("gu%d" % l, 0), load_unit("gu%d" % l, 1)]
                    for k in range(NCH):
                        for f in range(2):
                            wv = ring[:, sl2[f], :].rearrange("p (k c) -> p k c", k=NCH)
                            mm(f, wv[:, k, 0:128], xn[:, k, 0:tw], k == 0, k == NCH - 1, [("ring", sl2[f]), ("xn", k)],
                               out_ap=ps[f][:, 0:tw])
                            mm(2 + f, wv[:, k, 128:256], xn[:, k, 0:tw], k == 0, k == NCH - 1,
                               [("ring", sl2[f]), ("xn", k)], out_ap=ps[2 + f][:, 0:tw])
                for f in range(NFH):
                    bg, bu = f % 2, 2 + f % 2
                    if not (half == 0 and f < 2):
                        slot = load_unit("gu%d" % l, half * NFH + f)
                        wv = ring[:, slot, :].rearrange("p (k c) -> p k c", k=NCH)
                        for k in range(NCH):
                            mm(bg, wv[:, k, 0:128], xn[:, k, 0:tw], k == 0, k == NCH - 1, [("ring", slot), ("xn", k)],
                               out_ap=ps[bg][:, 0:tw])
                        for k in range(NCH):
                            mm(bu, wv[:, k, 128:256], xn[:, k, 0:tw], k == 0, k == NCH - 1, [("ring", slot), ("xn", k)],
                               out_ap=ps[bu][:, 0:tw])
                    flush_stat()
                    fi = nxt("fb", NFB)
                    act(fb[fi][:, 0:tw], ps[bg][:, 0:tw], AF.Silu, [("ps", bg)], [("fb", fi), ("ps", bg)])
                    S.add("dve", (lambda e, f=f, fi=fi, bu=bu: e.tensor_tensor(out=big[:, f, 0:tw], in0=ps[bu][:, 0:tw],
                                                                              in1=fb[fi][:, 0:tw], op=ALU.mult)),
                          reads=[("ps", bu), ("fb", fi)], writes=[("big", f), ("ps", bu)])
                for m in range(NCH):
                    slot = load_unit("dn%d" % l, half * NCH + m)
                    wv = ring[:, slot, 0:NFH * 128].rearrange("p (k c) -> p k c", k=NFH)
                    bank = 4 + m % 2
                    for k in range(NFH):
                        mm(bank, wv[:, k, :], big[:, k, 0:tw], k == 0, k == NFH - 1, [("ring", slot), ("big", k)],
                           out_ap=ps[bank][:, 0:tw])
                    if half == 0:
                        act(v[:, m, 0:tw], ps[bank][:, 0:tw], AF.Copy, [("ps", bank)], [("v", m), ("ps", bank)])
                    else:
                        S.add("dve", (lambda e, m=m, bank=bank: e.tensor_tensor(out=v[:, m, 0:tw], in0=ps[bank][:, 0:tw],
                                                                               in1=v[:, m, 0:tw], op=ALU.add)),
                              reads=[("ps", bank), ("v", m)], writes=[("v", m), ("ps", bank)])
                        i = nxt("sq", NSQ)
                        act(sqb[i][:, 0:tw], v[:, m, 0:tw], AF.Square, [("v", m)], [("sq", i)])
                        flush_stat()
                        pending_stat.append(stat_mm(7, i, tw, m == 0, m == NCH - 1, "sq"))
            post_norm_residual("g_ffn_post%d" % l, tw)

        def attn_kv(t, tw):
            def evac_k(bank, g):
                if t == 0:
                    act(kmeta[:, g, :], ps[bank][:, 0:NMETA], AF.Identity, [("ps", bank)], [("kmeta", g), ("ps", bank)],
                        bias=col("b_k", g))
                    act(kbuf[:, g, 3 * 128:4 * 128], ps[bank][:, T0 - 128:T0], AF.Identity, [("ps", bank)],
                        [("kbuf", g, 3), ("ps", bank)], bias=col("b_k", g))
                else:
                    s0 = 4 * (t % 2)
                    act(kbuf[:, g, s0 * 128:(s0 + 4) * 128], ps[bank][:, 0:TW], AF.Identity, [("ps", bank)],
                        [("kbuf", g, s0 + i) for i in range(4)] + [("ps", bank)], bias=col("b_k", g))
            proj_fm("qkv", 8, 4, lambda k: xn[:, k, 0:tw], lambda k: ("xn", k), tw, evac_k)
            slot = load_unit("qkv", 10)
            wv = ring[:, slot, :].rearrange("p (k c) -> p k c", k=NCH)
            if t == 0:
                blocks = [(T0 - 128, 128, 3, 0), (0, NMETA, None, 0), (0, NMETA, None, 32)]
            else:
                blocks = [(i * 128, 128, 4 * (t % 2) + i, 0) for i in range(4)]
            for bi, (o, n, vs, pr) in enumerate(blocks):
                bank = 4 + bi % 2
                for k in range(NCH):
                    mm(bank, xn[:, k, o:o + n], wv[:, k, :], k == 0, k == NCH - 1, [("ring", slot), ("xn", k)],
                       out_ap=ps[bank][pr:pr + n, 0:256])
                if vs is None:
                    S.add("dve", (lambda e, bank=bank, pr=pr: e.tensor_tensor(
                        out=vmeta[pr:pr + NMETA, :], in0=ps[bank][pr:pr + NMETA, 0:256],
                        in1=bv[pr:pr + NMETA, :], op=ALU.add)),
                        reads=[("ps", bank), "bv"], writes=["vmeta", ("ps", bank)])
                else:
                    S.add("dve", (lambda e, bank=bank, vs=vs: e.tensor_tensor(out=vbuf[:, vs, :], in0=ps[bank][:, 0:256],
                                                                             in1=bv[:, :], op=ALU.add)),
                          reads=[("ps", bank), "bv"], writes=[("vbuf", vs), ("ps", bank)])

        def attn_full(t):
            tw = TW
            rmsnorm_to_xn("g_mix_pre1", tw)

            def evac_q(bank, m):
                act(big[:, m, 0:tw], ps[bank][:, 0:tw], AF.Identity, [("ps", bank)], [("big", m), ("ps", bank)],
                    bias=col("bq_s", m), scale=0.125)
            proj_fm("qkv", 0, NCH, lambda k: xn[:, k, 0:tw], lambda k: ("xn", k), tw, evac_q, kouter=True)
            attn_kv(t, tw)
            steps = [(i, g, par) for i in range(4) for g in range(4) for par in range(2)]
            NS = len(steps)

            def emit_scores(n):
                i, g, par = steps[n]
                st = n % 2
                sc = 4 * (t % 2) + i
                sp_ = (sc - 1) % 8
                rows = slice(par * 64, par * 64 + 64)
                qs = slice(i * 128, (i + 1) * 128)
                qap = big[rows, 4 * g:4 * g + 4, qs]
                qres = [("big", 4 * g + j) for j in range(4)]
                mm(2 * st, kbuf[rows, g, sp_ * 128:(sp_ + 1) * 128], qap, True, False, [("kbuf", g, sp_)] + qres)
                mm(2 * st, ident[:], expb_p[:, par, 4 * g:4 * g + 4, :], False, True, ["ident", "expb_p"])
                mm(2 * st + 1, kbuf[rows, g, sc * 128:(sc + 1) * 128], qap, True, False, [("kbuf", g, sc)] + qres)
                mm(2 * st + 1, ident[:], expb_c[:, par, 4 * g:4 * g + 4, :], False, True, ["ident", "expb_c"])
                mr = 32 * st
                S.add("pe", lambda e: e.matmul(ps[4][mr:mr + NMETA, :], lhsT=kmeta[rows, g, :], rhs=qap, start=True, stop=True),
                      reads=[("kmeta", g)] + qres, writes=[("ps4", st)] + ([("ps", 4)] if n < 2 else []))

            def emit_softmax(n):
                i, g, par = steps[n]
                st = n % 2
                first = (t == 1 and i == 0)
                pis = []
                for kb in (0, 1):
                    bk = 2 * st + kb
                    pi = nxt("pb", NP)
                    act(pb[pi][:], ps[bk][:], AF.Exp, [("ps", bk)], [("pb", pi), ("ps", bk)])
                    if first and kb == 0:
                        S.add("dve", (lambda e, pi=pi: e.tensor_scalar(out=pb[pi][:], in0=pb[pi][:],
                                                                       scalar1=col("flag"), scalar2=None,
                                                                       op0=ALU.mult)),
                              reads=[("pb", pi)] + CONST, writes=[("pb", pi)])
                    pis.append(pi)
                mr = 32 * st
                mrows = slice(mr, mr + NMETA)
                ei = nxt("fb", NFB)
                act(fb[ei][mrows, :], ps[4][mrows, :], AF.Exp, [("ps4", st)],
                    [("fb", ei), ("ps4", st)] + ([("ps", 4)] if n >= NS - 2 else []))
                mi = nxt("pb", NP)
                if first:
                    xi = nxt("fb", NFB)
                    S.add("pool", (lambda e, xi=xi: e.dma_start(
                        out=fb[xi][mrows, :].rearrange("p (j q) -> p j q", j=4),
                        in_=mef_d[:, par, 4 * g:4 * g + 4, :])),
                        writes=[("fb", xi)], dma=("mef", xi))
                    act(fb[xi][mrows, :], fb[xi][mrows, :], AF.Exp, [("fb", xi)], [("fb", xi)])
                    S.add("dve", (lambda e, mi=mi, ei=ei, xi=xi: e.tensor_tensor(
                        out=pb[mi][mrows, :], in0=fb[ei][mrows, :], in1=fb[xi][mrows, :], op=ALU.mult)),
                        reads=[("fb", ei), ("fb", xi)], writes=[("pb", mi)])
                else:
                    S.add("dve", (lambda e, mi=mi, ei=ei: e.tensor_tensor(
                        out=pb[mi][mrows, :].rearrange("p (j q) -> p j q", j=4),
                        in0=fb[ei][mrows, :].rearrange("p (j q) -> p j q", j=4),
                        in1=mg[mrows, par * 16 + 4 * g:par * 16 + 4 * g + 4].unsqueeze(2).broadcast_to([NMETA, 4, 128]),
                        op=ALU.mult)),
                        reads=[("fb", ei), "mg"], writes=[("pb", mi)])
                return pis, mi

            def emit_pv(n, pis, mi):
                i, g, par = steps[n]
                st = n % 2
                sc = 4 * (t % 2) + i
                sp_ = (sc - 1) % 8
                rows = slice(par * 64, par * 64 + 64)
                qs = slice(i * 128, (i + 1) * 128)
                mrows = slice(32 * st, 32 * st + NMETA)
                vcols = slice(g * 64, (g + 1) * 64)
                nb = 5 + (i * 4 + g) % 2
                for bank, lt in ((nb, "v"), (7, "o")):
                    o_ap = ps[bank][rows, :]
                    l_p = vbuf[:, sp_, vcols] if lt == "v" else ones[:, 0:64]
                    l_c = vbuf[:, sc, vcols] if lt == "v" else ones[:, 0:64]
                    l_m = vmeta[mrows, vcols] if lt == "v" else ones[mrows, 0:64]
                    mm(bank, l_p, pb[pis[0]][:], True, False, [("vbuf", sp_), "ones", ("pb", pis[0])], out_ap=o_ap)
                    mm(bank, l_c, pb[pis[1]][:], False, False, [("vbuf", sc), "ones", ("pb", pis[1])], out_ap=o_ap)
                    mm(bank, l_m, pb[mi][mrows, :], False, True, ["vmeta", "ones", ("pb", mi)], out_ap=o_ap)
                if par == 1:
                    fi = nxt("fb", NFB)
                    S.add("dve", (lambda e, fi=fi: e.tensor_tensor(
                        out=fb[fi][:].rearrange("p (j q) -> p j q", j=4),
                        in0=ps[7][:].rearrange("p (j q) -> p j q", j=4),
                        in1=col("es", 4 * g, 4).unsqueeze(2).broadcast_to([128, 4, 128]), op=ALU.add)),
                        reads=[("ps", 7)] + CONST, writes=[("fb", fi), ("ps", 7)])
                    act(fb[fi][:], fb[fi][:], AF.Ln, [("fb", fi)], [("fb", fi)])
                    act(fb[fi][:], fb[fi][:], AF.Exp, [("fb", fi)], [("fb", fi)], scale=-1.0)
                    S.add("dve", (lambda e, fi=fi: e.tensor_tensor(
                        out=big[:, 4 * g:4 * g + 4, qs],
                        in0=ps[nb][:].rearrange("p (j q) -> p j q", j=4),
                        in1=fb[fi][:].rearrange("p (j q) -> p j q", j=4), op=ALU.mult)),
                        reads=[("ps", nb), ("fb", fi)], writes=[("big", 4 * g + j) for j in range(4)] + [("ps", nb)])

            emit_scores(0)
            for n in range(NS):
                if n + 1 < NS:
                    emit_scores(n + 1)
                pis, mi = emit_softmax(n)
                emit_pv(n, pis, mi)
            proj_fm("wo", 0, NCH, lambda k: big[:, k, 0:tw], lambda k: ("big", k), tw,
                    lambda bank, m: evac_v_and_stat(bank, m, tw, col("b_o", m), m == 0, m == NCH - 1))
            post_norm_residual("g_mix_post1", tw)

        allh = [("h", c) for c in range(NCH)]
        for t in range(NT + 1):
            S.seg = t + 1
            tw = T0 if t == 0 else TW
            off = 0 if t == 0 else T0 + (t - 1) * TW
            for c4 in range(0, NCH, 4):
                S.add("sp", (lambda e, off=off, tw=tw, c4=c4: e.dma_start(out=h[:, c4:c4 + 4, 0:tw],
                                                                          in_=xin[:, c4:c4 + 4, off:off + tw])),
                      writes=[("h", c) for c in range(c4, c4 + 4)], dma=("xin", c4))
            conv_mixer(t, tw)
            ffn(0, tw)
            if t == 0:
                emit_casts(("qkv",), extra_reads=[("h", 0)], sel=lambda k, u: u < 8)
                emit_casts(LATE, extra_reads=[("h", 0)])
                rmsnorm_to_xn("g_mix_pre1", tw)
                attn_kv(0, tw)
            else:
                attn_full(t)
                ffn(1, tw)
                for c4 in range(0, NCH, 4):
                    S.add("pool", (lambda e, t=t, c4=c4: e.dma_start(out=out_d[:, c4:c4 + 4, (t - 1) * TW:t * TW],
                                                                     in_=h[:, c4:c4 + 4, :])),
                          reads=[("h", c) for c in range(c4, c4 + 4)], dma=("out", c4))
        S.add("sp", lambda e: None, reads=allh + ["outdone"], writes=allh)
        S.emit(nc, es)
    return nc


def _t5_bucket(dist):
    dist = np.asarray(dist, np.int64)
    d = np.maximum(dist, 16).astype(np.float32)
    large = 16 + (np.log(d / np.float32(16.0)) / np.float32(math.log(8.0)) * np.float32(16.0)).astype(np.int32)
    return np.where(dist < 16, dist, np.minimum(large, 31)).astype(np.int64)


def _vec_cols(vv):
    return np.ascontiguousarray(np.asarray(vv, np.float32).reshape(-1, 128).T)


def _img(w, ncols):
    k = w.shape[0] // 128
    return np.ascontiguousarray(w.reshape(k, 128, ncols).transpose(1, 0, 2)).reshape(128, k * ncols)


def _prep_shared(inp):
    f = lambda k: np.asarray(inp[k], np.float32)
    sh = {}
    w_in = f("conv_w_in")[0]
    sh["w_cin"] = np.stack([_img(np.concatenate([w_in[:, c * 128:(c + 1) * 128],
                                                 w_in[:, D + c * 128:D + (c + 1) * 128]], 1), 256) for c in range(16)])
    wdw_ = f("conv_w_dw")[0]
    dg = np.zeros((16, 128, 32, 128), np.float32)
    pp = np.arange(128)
    for c in range(16):
        for j in range(CW):
            dg[c, pp, j, pp] = wdw_[j, c * 128 + pp]
    sh["w_dg"] = dg.reshape(16, 128, 4096)
    w_out = f("conv_w_out")[0]
    sh["w_cout"] = np.stack([_img(w_out[:, u * 256:(u + 1) * 256], 256) for u in range(8)])
    for l in range(2):
        wg, wu, wd = f("ffn_w_gate")[l], f("ffn_w_up")[l], f("ffn_w_down")[l]
        sh["w_gu%d" % l] = np.stack([_img(np.concatenate([wg[:, i * 128:(i + 1) * 128],
                                                          wu[:, i * 128:(i + 1) * 128]], 1), 256) for i in range(NF)])
        sh["w_dn%d" % l] = np.stack([_img(wd[half * NFH * 128:(half + 1) * NFH * 128, m * 128:(m + 1) * 128], 128)
                                     for half in range(2) for m in range(16)])
    wqkv = f("attn_w_qkv")[0]
    units = [_img(wqkv[:, u * 256:(u + 1) * 256], 256) for u in range(8)]
    wk = wqkv[:, D:D + 256]
    wv = wqkv[:, D + 256:D + 512]
    for u in range(2):
        kv0, kv1 = 2 * u, 2 * u + 1
        units.append(_img(np.concatenate([wk[:, kv0 * 64:(kv0 + 1) * 64]] * 2 + [wk[:, kv1 * 64:(kv1 + 1) * 64]] * 2, 1), 256))
    units.append(_img(wv, 256))
    sh["w_qkv"] = np.stack(units)
    wo = f("attn_w_o")[0]
    sh["w_wo"] = np.stack([_img(wo[:, u * 256:(u + 1) * 256], 256) for u in range(8)])

    cvt = np.zeros((128, NCV), np.float32)

    def put(name, arr):
        arr = np.asarray(arr, np.float32)
        cvt[:, _cols[name]:_cols[name] + arr.shape[1]] = arr
    for l in range(2):
        put("g_mix_pre%d" % l, _vec_cols(f("norm_mix_pre")[l]))
        put("g_mix_post%d" % l, _vec_cols(f("norm_mix_post")[l]))
        put("g_ffn_pre%d" % l, _vec_cols(f("norm_ffn_pre")[l]))
        put("g_ffn_post%d" % l, _vec_cols(f("norm_ffn_post")[l]))
    b_in = f("conv_b_in")[0]
    put("b_in_a", _vec_cols(b_in[:D]))
    put("b_in_g", _vec_cols(b_in[D:]))
    put("b_dw", _vec_cols(f("conv_b_dw")[0]))
    put("ln_g", _vec_cols(f("conv_ln_g")[0]))
    put("ln_b", _vec_cols(f("conv_ln_b")[0]))
    put("b_out", _vec_cols(f("conv_b_out")[0]))
    bqkv = f("attn_b_qkv")[0]
    put("b_q", _vec_cols(bqkv[:D]))
    put("b_o", _vec_cols(f("attn_b_o")[0]))
    bk = bqkv[D:D + 256].reshape(4, 64)
    put("b_k", np.concatenate([bk, bk], 1).T)
    sinks = f("attn_sinks")[0]
    put("sinks", np.stack([sinks[2 * np.arange(16) + (1 if p >= 64 else 0)] for p in range(128)]))
    wdw = f("conv_w_dw")[0]
    put("w_dw", np.concatenate([_vec_cols(wdw[j]) for j in range(CW)], 1))
    sh["cv"] = cvt
    sh["bv_rep"] = np.ascontiguousarray(np.broadcast_to(bqkv[D + 256:D + 512][None, :], (128, 256)))
    rb = f("rel_bias")
    sh["mg"] = np.ascontiguousarray(np.broadcast_to(
        np.concatenate([rb[31, 0::2], rb[31, 1::2]])[None, :], (16, 32)))
    kk = np.arange(128)[:, None]
    qq = np.arange(128)[None, :]
    hd = (2 * np.arange(16)[None, :] + np.arange(2)[:, None])

    def table(dist, valid):
        g = rb[_t5_bucket(np.maximum(dist, 0))]
        tab = g[:, :, hd]
        tab = np.where(valid[:, :, None, None], tab, np.float32(NEG))
        return np.ascontiguousarray(tab.transpose(0, 2, 3, 1)).reshape(dist.shape[0], -1).astype(np.float32)
    sh["ident"] = np.eye(128, dtype=np.float32)
    sh["bias_prev"] = table(128 + qq - kk, kk > qq)
    sh["bias_cur"] = table(qq - kk, kk <= qq)
    mm_ = np.arange(NMETA)[:, None]
    sh["_mef_first"] = table(NMETA + qq - mm_, np.ones((NMETA, 128), bool)).reshape(NMETA, 2, 16, 128)
    sh["_mef_gen"] = table(np.full((NMETA, 128), 1000), np.ones((NMETA, 128), bool)).reshape(NMETA, 2, 16, 128)
    return sh


def _run(inp, cores_per_batch, nt):
    x = np.asarray(inp["x"], np.float32)
    B, S_, _ = x.shape
    assert S_ == cores_per_batch * nt * TW
    meta = np.asarray(inp["meta_tokens"], np.float32)
    sh = _prep_shared(inp)
    mef_first = sh.pop("_mef_first")
    mef_gen = sh.pop("_mef_gen")
    in_maps = []
    for b in range(B):
        for q in range(cores_per_batch):
            s0 = q * nt * TW
            if q == 0:
                halo = np.concatenate([np.zeros((HALO - NMETA, D), np.float32), meta], 0)
            else:
                halo = x[b, s0 - HALO:s0]
            stream = np.concatenate([meta, halo, x[b, s0:s0 + nt * TW]], 0)
            ntok = stream.shape[0]
            xin = np.ascontiguousarray(stream.T.reshape(16, 128, ntok).transpose(1, 0, 2))
            mask0 = np.ones((128, T0), np.float32)
            cvt = sh["cv"].copy()
            if q == 0:
                mask0[:, NMETA:T0 - NMETA] = 0.0
                cvt[:, _cols["flag"]] = 0.0
            else:
                cvt[:, _cols["flag"]] = 1.0
            m = dict(sh)
            m["cv"] = cvt
            m["xin"] = xin
            m["mask0"] = mask0
            m["mef"] = np.ascontiguousarray(mef_first if q == 0 else mef_gen)
            in_maps.append(m)
    nc = build_program(nt)
    res = run_bass_kernel_spmd(nc, in_maps, core_ids=list(range(len(in_maps))))
    out = np.empty((B, S_, D), np.float32)
    i = 0
    for b in range(B):
        for q in range(cores_per_batch):
            o = np.asarray(res.results[i]["out"])
            out[b, q * nt * TW:(q + 1) * nt * TW] = o.transpose(2, 1, 0).reshape(nt * TW, D)
            i += 1
    return out


def kernel(**inputs):
    return _run(inputs, cores_per_batch=4, nt=8)
```

```python
import math
from contextlib import ExitStack

import numpy as np
import concourse.bass as bass
import concourse.mybir as mybir
from concourse.bass_utils import run_bass_kernel_spmd

F32 = mybir.dt.float32
BF16 = mybir.dt.bfloat16
AF = mybir.ActivationFunctionType
ALU = mybir.AluOpType

D = 2048
NCH = 16
DFF = 5632
NF = 44
NFH = 22
NMETA = 16
HALO = 176
T0 = NMETA + HALO
TW = 512
CW = 31
NHEADS = 32
RMS_EPS = 1e-6
LN_EPS = 1e-5
NEG = -30000.0
NSLOT = 4
SLOT_EL = 4096

_cols = {}
_n = 0
for _name, _w in [("g_mix_pre0", 16), ("g_mix_post0", 16), ("g_ffn_pre0", 16), ("g_ffn_post0", 16),
                  ("g_mix_pre1", 16), ("g_mix_post1", 16), ("g_ffn_pre1", 16), ("g_ffn_post1", 16),
                  ("b_in_a", 16), ("b_in_g", 16), ("b_dw", 16), ("ln_g", 16), ("ln_b", 16),
                  ("b_out", 16), ("b_q", 16), ("b_o", 16), ("b_k", 4), ("sinks", 16),
                  ("w_dw", CW * 16), ("flag", 1), ("bq_s", 16), ("es", 16)]:
    _cols[_name] = _n
    _n += _w
NCV = _n


class Op:
    __slots__ = ("eng", "fn", "deps", "signal", "sigval", "seg", "pos", "is_dma", "dkey", "dval", "idx")


class Sched:
    ENGS = ("pe", "act", "dve", "pool", "sp")

    def __init__(self):
        self.ops = []
        self.by_eng = {e: [] for e in self.ENGS}
        self.last_w = {}
        self.readers = {}
        self.seg = 0
        self.dma_count = {}

    def add(self, eng, fn, reads=(), writes=(), dma=None, nodep_writes=()):
        op = Op()
        op.eng = eng
        op.fn = fn
        op.signal = False
        op.sigval = 0
        op.seg = self.seg
        op.is_dma = dma is not None
        op.dkey = dma
        op.idx = len(self.ops)
        deps = {}
        for r in reads:
            w = self.last_w.get(r)
            if w is not None:
                deps[w.idx] = w
        for r in writes:
            w = self.last_w.get(r)
            if w is not None:
                deps[w.idx] = w
            for o in self.readers.get(r, {}).values():
                deps[o.idx] = o
        for r in reads:
            self.readers.setdefault(r, {})[eng if not op.is_dma else ("dma", op.idx)] = op
        for r in list(writes) + list(nodep_writes):
            self.last_w[r] = op
            self.readers[r] = {}
        deps.pop(op.idx, None)
        op.deps = list(deps.values())
        if op.is_dma:
            self.dma_count[dma] = self.dma_count.get(dma, 0) + 1
            op.dval = 16 * self.dma_count[dma]
        op.pos = len(self.by_eng[eng])
        self.by_eng[eng].append(op)
        self.ops.append(op)
        return op

    def _needs_wait(self, op, d):
        if d.is_dma:
            return True
        if d.eng == op.eng:
            if op.is_dma:
                return True
            if op.eng == "pe":
                return False
            return (op.pos - d.pos) <= 1
        return True

    def emit(self, nc, es):
        for op in self.ops:
            for d in op.deps:
                if not d.is_dma and self._needs_wait(op, d):
                    d.signal = True
        cnt = {}
        for op in self.ops:
            if op.signal and not op.is_dma:
                k = (op.eng, op.seg)
                cnt[k] = cnt.get(k, 0) + 1
                op.sigval = cnt[k]
        esem = {k: es.enter_context(nc.semaphore("s_%s_%d" % k)) for k in cnt}
        dsem = {}
        for i, k in enumerate(self.dma_count):
            dsem[k] = es.enter_context(nc.semaphore("d_%d" % i))
        block = es.enter_context(nc.Block())

        def run(e, name):
            waited = {}
            for op in self.by_eng[name]:
                for d in op.deps:
                    if not self._needs_wait(op, d):
                        continue
                    if d.is_dma:
                        sem, val = dsem[d.dkey], d.dval
                    else:
                        sem, val = esem[(d.eng, d.seg)], d.sigval
                    key = id(sem)
                    if waited.get(key, 0) >= val:
                        continue
                    e.wait_ge(sem, val)
                    waited[key] = val
                ins = op.fn(e)
                if ins is None:
                    continue
                if op.is_dma:
                    ins.then_inc(dsem[op.dkey], 16)
                elif op.signal:
                    ins.then_inc(esem[(op.eng, op.seg)], 1)

        @block.tensor
        def _(e):
            run(e, "pe")

        @block.scalar
        def _(e):
            run(e, "act")

        @block.vector
        def _(e):
            run(e, "dve")

        @block.gpsimd
        def _(e):
            run(e, "pool")

        @block.sync
        def _(e):
            run(e, "sp")


def build_program(NT):
    NTOK = T0 + NT * TW
    nc = bass.Bass("TRN2", target_bir_lowering=False)
    S = Sched()

    def din(name, shape, dt=F32):
        return nc.dram_tensor(name, list(shape), dt, kind="ExternalInput").ap()

    xin = din("xin", [128, NCH, NTOK])
    cv_d = din("cv", [128, NCV])
    bv_d = din("bv_rep", [128, 256])
    mask_d = din("mask0", [128, T0])
    mg_d = din("mg", [16, 32])
    bprev_d = din("bias_prev", [128, 4096])
    bcur_d = din("bias_cur", [128, 4096])
    ident_d = din("ident", [128, 128])
    mef_d = din("mef", [16, 2, NCH, 128])
    out_d = nc.dram_tensor("out", [128, NCH, NT * TW], F32, kind="ExternalOutput").ap()

    streams = {"cin": (16, 4096), "dg": (16, 4096), "cout": (8, 4096), "gu0": (NF, 4096), "dn0": (32, 2816),
               "qkv": (11, 4096), "wo": (8, 4096), "gu1": (NF, 4096), "dn1": (32, 2816)}
    w32 = {}
    wbf = {}
    for k, (nu, x) in streams.items():
        if k != "dg":
            w32[k] = din("w_" + k, [nu, 128, x])
        wbf[k] = nc.dram_tensor("wb_" + k, [nu, 128, x], BF16, kind="Internal").ap()

    with ExitStack() as es:
        def sb(name, shape, dt):
            return es.enter_context(nc.sbuf_tensor(name, list(shape), dt))

        h = sb("h", [128, NCH, TW], F32)
        xn = sb("xn", [128, NCH, TW], BF16)
        v = sb("v", [128, NCH, TW], F32)
        big = sb("big", [128, NFH, TW], BF16)
        ring = sb("ring", [128, NSLOT, SLOT_EL], BF16)
        kbuf = sb("kbuf", [128, 4, 8 * 128], BF16)
        vbuf = sb("vbuf", [128, 8, 256], BF16)
        kmeta = sb("kmeta", [128, 4, NMETA], BF16)
        vmeta = sb("vmeta", [48, 256], BF16)
        expb_p = sb("expb_p", [128, 2, NCH, 128], BF16)
        expb_c = sb("expb_c", [128, 2, NCH, 128], BF16)
        ident = sb("ident_t", [128, 128], BF16)
        cv = sb("cvt", [128, NCV], F32)
        bv = sb("bvt", [128, 256], F32)
        mask0 = sb("mask0t", [128, T0], F32)
        mg = sb("mgt", [48, 32], F32)
        ones = sb("ones", [128, 128], BF16)
        ucarry = sb("ucarry", [128, NCH, CW - 1], BF16)
        NU = 4
        ub = [sb("ub%d" % i, [128, CW - 1 + TW], BF16) for i in range(NU)]
        NSQ = 4
        sqb = [sb("sqb%d" % i, [128, TW], BF16) for i in range(NSQ)]
        ybfb = [sb("ybf%d" % i, [128, TW], BF16) for i in range(2)]
        NFB = 4
        fb = [sb("fb%d" % i, [128, TW], F32) for i in range(NFB)]
        st1 = sb("st1", [128, TW], F32)
        st2 = sb("st2", [128, TW], F32)
        st3 = sb("st3", [128, TW], F32)
        NP = 8
        pb = [sb("pb%d" % i, [128, TW], BF16) for i in range(NP)]
        ps = [es.enter_context(nc.psum_tensor("ps%d" % i, [128, TW], F32)) for i in range(8)]

        def col(name, c=0, n=1):
            o = _cols[name] + c
            return cv[:, o:o + n]

        cnt = {"u": 0, "sq": 0, "fb": 0, "pb": 0, "pm": 0, "ybf": 0, "ub": 0}

        def nxt(k, n):
            i = cnt[k] % n
            cnt[k] += 1
            return i

        S.seg = 0
        S.add("sp", lambda e: e.dma_start(out=cv[:], in_=cv_d), writes=["cv"], dma="c_cv")
        S.add("sp", lambda e: e.dma_start(out=bv[:], in_=bv_d), writes=["bv"], dma="c_bv")
        S.add("sp", lambda e: e.dma_start(out=mask0[:], in_=mask_d), writes=["mask0"], dma="c_mask")
        S.add("sp", lambda e: e.dma_start(out=mg[0:16, :], in_=mg_d), nodep_writes=["mg"], dma="c_mg")
        S.add("sp", lambda e: e.dma_start(out=mg[32:48, :], in_=mg_d), nodep_writes=["mg"], dma="c_mg")
        ebp = expb_p[:].rearrange("p a c q -> p (a c q)")
        ebc = expb_c[:].rearrange("p a c q -> p (a c q)")
        S.add("pool", lambda e: e.dma_start(out=ebp, in_=bprev_d), writes=["expb_p"], dma="c_bp")
        S.add("pool", lambda e: e.dma_start(out=ebc, in_=bcur_d), writes=["expb_c"], dma="c_bc")
        S.add("pool", lambda e: e.dma_start(out=ident[:], in_=ident_d), writes=["ident"], dma="c_id")
        LATE = ("wo", "gu1", "dn1")
        FINE = ("cin",)

        def emit_casts(keys, extra_reads=()):
            for k in keys:
                nu = streams[k][0]
                for u in range(nu):
                    fine = k in FINE
                    S.add("pool", (lambda e, k=k, u=u: e.dma_start(out=wbf[k][u], in_=w32[k][u])),
                          reads=list(extra_reads), nodep_writes=[("wbf", k, u) if fine else ("wbf", k)],
                          dma=("cast", k, u) if fine else ("cast", k))

        emit_casts([k for k in streams if k not in LATE and k != "dg"])
        S.add("dve", lambda e: e.memset(ones[:], 1.0), writes=["ones"])
        S.add("dve", lambda e: e.memset(ucarry[:], 0.0), writes=[("ucarry", c) for c in range(NCH)])
        S.add("dve", lambda e: e.tensor_scalar(out=col("bq_s", 0, 16), in0=col("b_q", 0, 16), scalar1=0.125,
                                               scalar2=None, op0=ALU.mult), reads=["cv"], writes=["cv2"])
        S.add("act", lambda e: e.activation(out=col("es", 0, 16), in_=col("sinks", 0, 16), func=AF.Exp),
              reads=["cv"], writes=["cv3"])
        S.add("act", lambda e: e.activation(out=mg[0:16, :], in_=mg[0:16, :], func=AF.Exp), reads=["mg"], writes=["mg"])
        S.add("act", lambda e: e.activation(out=mg[32:48, :], in_=mg[32:48, :], func=AF.Exp), reads=["mg"], writes=["mg"])
        for c in range(NCH):
            dslot = c % NSLOT
            for j in range(CW):
                S.add("act", (lambda e, dslot=dslot, j=j, c=c: e.activation(
                    out=ring[:, dslot, j * 128:(j + 1) * 128], in_=ident[:], func=AF.Copy,
                    scale=col("w_dw", j * 16 + c))),
                    reads=["ident", "cv"], writes=[("ring", dslot)])
            S.add("sp", (lambda e, dslot=dslot, c=c: e.dma_start(out=wbf["dg"][c][:, 0:CW * 128],
                                                                in_=ring[:, dslot, 0:CW * 128])),
                  reads=[("ring", dslot)], nodep_writes=[("wbf", "dg")], dma=("cast", "dg"))
        CONST = ["cv", "cv2", "cv3"]

        unit_ctr = [0]

        def load_unit(stream, u):
            slot = unit_ctr[0] % NSLOT
            unit_ctr[0] += 1
            x = streams[stream][1]
            S.add("sp", lambda e: e.dma_start(out=ring[:, slot, 0:x], in_=wbf[stream][u]),
                  reads=[("wbf", stream, u) if stream in FINE else ("wbf", stream)], writes=[("ring", slot)],
                  dma=("ring", slot))
            return slot

        def mm(bank, lhsT, rhs, start, stop, reads, out_ap=None):
            o = ps[bank][:] if out_ap is None else out_ap
            S.add("pe", lambda e: e.matmul(o, lhsT=lhsT, rhs=rhs, start=start, stop=stop),
                  reads=reads, writes=[("ps", bank)])

        def act(out, in_, func, reads, writes, bias=None, scale=None):
            kw = {}
            if bias is not None:
                kw["bias"] = bias
            if scale is not None:
                kw["scale"] = scale
            S.add("act", lambda e: e.activation(out=out, in_=in_, func=func, **kw),
                  reads=list(reads) + CONST, writes=writes)

        pending_stat = []

        def flush_stat():
            while pending_stat:
                pending_stat.pop(0)()

        def stat_mm(bank, sqi, tw, first, last, src_key):
            def f():
                mm(bank, ones[:], sqb[sqi][:, 0:tw] if src_key == "sq" else ybfb[sqi][:, 0:tw], first, last,
                   ["ones", (src_key, sqi)], out_ap=ps[bank][:, 0:tw])
            return f

        def rstd_from(bank, tw, eps):
            act(st1[:, 0:tw], ps[bank][:, 0:tw], AF.Ln, [("ps", bank)], ["st1", ("ps", bank)],
                bias=eps_t[eps], scale=1.0 / D)
            act(st2[:, 0:tw], st1[:, 0:tw], AF.Exp, ["st1"], ["st2"], scale=-0.5)

        eps_t = {}
        for nm, val in (("rms", RMS_EPS), ("ln", LN_EPS)):
            t_ = sb("eps_" + nm, [128, 1], F32)
            S.add("dve", (lambda e, t_=t_, val=val: e.memset(t_[:], val)), writes=["eps_" + nm])
            eps_t[nm] = t_[:]
            CONST.append("eps_" + nm)

        def rmsnorm_to_xn(gname, tw, split=False):
            for c in range(NCH):
                i = nxt("sq", NSQ)
                if c % 2 == 0 or not split:
                    act(sqb[i][:, 0:tw], h[:, c, 0:tw], AF.Square, [("h", c)], [("sq", i)])
                else:
                    S.add("dve", (lambda e, i=i, c=c: e.tensor_tensor(out=sqb[i][:, 0:tw], in0=h[:, c, 0:tw],
                                                                      in1=h[:, c, 0:tw], op=ALU.mult)),
                          reads=[("h", c)], writes=[("sq", i)])
                mm(7, ones[:], sqb[i][:, 0:tw], c == 0, c == NCH - 1, ["ones", ("sq", i)], out_ap=ps[7][:, 0:tw])
            rstd_from(7, tw, "rms")
            for c in range(NCH):
                S.add("dve", (lambda e, c=c: e.scalar_tensor_tensor(out=xn[:, c, 0:tw], in0=h[:, c, 0:tw],
                                                                    scalar=col(gname, c), in1=st2[:, 0:tw],
                                                                    op0=ALU.mult, op1=ALU.mult)),
                      reads=[("h", c), "st2"] + CONST, writes=[("xn", c)])

        def post_norm_residual(gname, tw):
            flush_stat()
            rstd_from(7, tw, "rms")
            for c in range(NCH):
                i = nxt("fb", NFB)
                S.add("dve", (lambda e, c=c, i=i: e.scalar_tensor_tensor(out=fb[i][:, 0:tw], in0=v[:, c, 0:tw],
                                                                         scalar=col(gname, c), in1=st2[:, 0:tw],
                                                                         op0=ALU.mult, op1=ALU.mult)),
                      reads=[("v", c), "st2"] + CONST, writes=[("fb", i)])
                S.add("dve", (lambda e, c=c, i=i: e.tensor_tensor(out=h[:, c, 0:tw], in0=h[:, c, 0:tw],
                                                                  in1=fb[i][:, 0:tw], op=ALU.add)),
                      reads=[("h", c), ("fb", i)], writes=[("h", c)])

        def evac_v_and_stat(bank, m, tw, bias_ap, first, last):
            if bias_ap is not None:
                act(v[:, m, 0:tw], ps[bank][:, 0:tw], AF.Identity, [("ps", bank)], [("v", m), ("ps", bank)], bias=bias_ap)
            else:
                act(v[:, m, 0:tw], ps[bank][:, 0:tw], AF.Copy, [("ps", bank)], [("v", m), ("ps", bank)])
            i = nxt("sq", NSQ)
            act(sqb[i][:, 0:tw], v[:, m, 0:tw], AF.Square, [("v", m)], [("sq", i)])
            flush_stat()
            pending_stat.append(stat_mm(7, i, tw, first, last, "sq"))

        def proj_fm(stream, u0, n_out, rhs_of, rhs_res, tw, evac, cols_per_unit=2, kouter=False):
            slot = None
            m0 = 0
            if kouter and n_out >= 4:
                slots = [load_unit(stream, u0), load_unit(stream, u0 + 1)]
                for k in range(NCH):
                    for m in range(4):
                        sl = slots[m // 2]
                        wv = ring[:, sl, :].rearrange("p (k c) -> p k c", k=NCH)
                        lo = (m % 2) * 128
                        mm(m, wv[:, k, lo:lo + 128], rhs_of(k), k == 0, k == NCH - 1,
                           [("ring", sl), rhs_res(k)], out_ap=ps[m][:, 0:tw])
                for m in range(4):
                    evac(m, m)
                m0 = 4
            for m in range(m0, n_out):
                if m % cols_per_unit == 0:
                    slot = load_unit(stream, u0 + m // cols_per_unit)
                wv = ring[:, slot, :].rearrange("p (k c) -> p k c", k=NCH)
                lo = (m % cols_per_unit) * 128
                bank = m % 4
                for k in range(NCH):
                    mm(bank, wv[:, k, lo:lo + 128], rhs_of(k), k == 0, k == NCH - 1,
                       [("ring", slot), rhs_res(k)], out_ap=ps[bank][:, 0:tw])
                evac(bank, m)

        def conv_mixer(t, tw):
            rmsnorm_to_xn("g_mix_pre0", tw, split=True)

            def taps(c, ui):
                slot = load_unit("dg", c)
                dv = ring[:, slot, :].rearrange("p (j m) -> p j m", j=32)
                bank = 4 + c % 2
                for j in range(CW):
                    mm(bank, dv[:, j, :], ub[ui][:, j:j + tw], j == 0, j == CW - 1, [("ring", slot), ("ub", ui)],
                       out_ap=ps[bank][:, 0:tw])
                act(v[:, c, 0:tw], ps[bank][:, 0:tw], AF.Identity, [("ps", bank)], [("v", c), ("ps", bank)],
                    bias=col("b_dw", c))
                yi = nxt("ybf", 2)
                act(ybfb[yi][:, 0:tw], v[:, c, 0:tw], AF.Copy, [("v", c)], [("ybf", yi)])
                si = nxt("sq", NSQ)
                act(sqb[si][:, 0:tw], v[:, c, 0:tw], AF.Square, [("v", c)], [("sq", si)])
                pending_stat.append(stat_mm(6, yi, tw, c == 0, c == NCH - 1, "ybf"))
                pending_stat.append(stat_mm(7, si, tw, c == 0, c == NCH - 1, "sq"))

            prev = None
            sl2 = [load_unit("cin", 0), load_unit("cin", 1)]
            for k in range(NCH):
                for cc in range(2):
                    wv = ring[:, sl2[cc], :].rearrange("p (k c) -> p k c", k=NCH)
                    mm(cc, wv[:, k, 0:128], xn[:, k, 0:tw], k == 0, k == NCH - 1, [("ring", sl2[cc]), ("xn", k)],
                       out_ap=ps[cc][:, 0:tw])
                    mm(2 + cc, wv[:, k, 128:256], xn[:, k, 0:tw], k == 0, k == NCH - 1, [("ring", sl2[cc]), ("xn", k)],
                       out_ap=ps[2 + cc][:, 0:tw])
            for c in range(NCH):
                ba, bg = c % 2, 2 + c % 2
                if c >= 2:
                    slot = load_unit("cin", c)
                    wv = ring[:, slot, :].rearrange("p (k c) -> p k c", k=NCH)
                    for k in range(NCH):
                        mm(ba, wv[:, k, 0:128], xn[:, k, 0:tw], k == 0, k == NCH - 1, [("ring", slot), ("xn", k)],
                           out_ap=ps[ba][:, 0:tw])
                    for k in range(NCH):
                        mm(bg, wv[:, k, 128:256], xn[:, k, 0:tw], k == 0, k == NCH - 1, [("ring", slot), ("xn", k)],
                           out_ap=ps[bg][:, 0:tw])
                flush_stat()
                fi = nxt("fb", NFB)
                act(fb[fi][:, 0:tw], ps[bg][:, 0:tw], AF.Sigmoid, [("ps", bg)], [("fb", fi), ("ps", bg)],
                    bias=col("b_in_g", c))
                ui = nxt("ub", NU)
                S.add("pool", (lambda e, ui=ui, c=c: e.tensor_copy(out=ub[ui][:, 0:CW - 1], in_=ucarry[:, c, :])),
                      reads=[("ucarry", c)], writes=[("ub", ui)])
                S.add("dve", (lambda e, ui=ui, c=c, fi=fi, ba=ba: e.scalar_tensor_tensor(
                    out=ub[ui][:, CW - 1:CW - 1 + tw], in0=ps[ba][:, 0:tw], scalar=col("b_in_a", c),
                    in1=fb[fi][:, 0:tw], op0=ALU.add, op1=ALU.mult)),
                    reads=[("ps", ba), ("fb", fi)] + CONST, writes=[("ub", ui), ("ps", ba)])
                if t == 0:
                    S.add("dve", (lambda e, ui=ui: e.tensor_tensor(out=ub[ui][:, CW - 1:CW - 1 + tw],
                                                                   in0=ub[ui][:, CW - 1:CW - 1 + tw],
                                                                   in1=mask0[:, 0:tw], op=ALU.mult)),
                          reads=[("ub", ui), "mask0"], writes=[("ub", ui)])
                S.add("pool", (lambda e, ui=ui, c=c: e.tensor_copy(out=ucarry[:, c, :], in_=ub[ui][:, tw:tw + CW - 1])),
                      reads=[("ub", ui)], writes=[("ucarry", c)])
                if prev is not None:
                    taps(*prev)
                prev = (c, ui)
            taps(*prev)
            flush_stat()
            S.add("dve", lambda e: e.tensor_scalar(out=st1[:, 0:tw], in0=ps[6][:, 0:tw], scalar1=1.0 / D, scalar2=None,
                                                   op0=ALU.mult), reads=[("ps", 6)], writes=["st1", ("ps", 6)])
            S.add("dve", lambda e: e.tensor_tensor(out=st3[:, 0:tw], in0=st1[:, 0:tw], in1=st1[:, 0:tw], op=ALU.mult),
                  reads=["st1"], writes=["st3"])
            S.add("dve", lambda e: e.scalar_tensor_tensor(out=st3[:, 0:tw], in0=ps[7][:, 0:tw], scalar=1.0 / D,
                                                          in1=st3[:, 0:tw], op0=ALU.mult, op1=ALU.subtract),
                  reads=[("ps", 7), "st3"], writes=["st3", ("ps", 7)])
            act(st3[:, 0:tw], st3[:, 0:tw], AF.Ln, ["st3"], ["st3"], bias=eps_t["ln"])
            act(st2[:, 0:tw], st3[:, 0:tw], AF.Exp, ["st3"], ["st2"], scale=-0.5)
            for c in range(NCH):
                i = nxt("fb", NFB)
                S.add("dve", (lambda e, c=c, i=i: e.tensor_tensor(out=fb[i][:, 0:tw], in0=v[:, c, 0:tw],
                                                                  in1=st1[:, 0:tw], op=ALU.subtract)),
                      reads=[("v", c), "st1"], writes=[("fb", i)])
                S.add("dve", (lambda e, i=i: e.tensor_tensor(out=fb[i][:, 0:tw], in0=fb[i][:, 0:tw],
                                                             in1=st2[:, 0:tw], op=ALU.mult)),
                      reads=[("fb", i), "st2"], writes=[("fb", i)])
                act(xn[:, c, 0:tw], fb[i][:, 0:tw], AF.Silu, [("fb", i)], [("xn", c)],
                    bias=col("ln_b", c), scale=col("ln_g", c))
            proj_fm("cout", 0, NCH, lambda k: xn[:, k, 0:tw], lambda k: ("xn", k), tw,
                    lambda bank, m: evac_v_and_stat(bank, m, tw, col("b_out", m), m == 0, m == NCH - 1), kouter=True)
            post_norm_residual("g_mix_post0", tw)

        def ffn(l, tw):
            rmsnorm_to_xn("g_ffn_pre%d" % l, tw)
            for half in range(2):
                if half == 0:
                    sl2 = [load_unit("gu%d" % l, 0), load_unit("gu%d" % l, 1)]
                    for k in range(NCH):
                        for f in range(2):
                            wv = ring[:, sl2[f], :].rearrange("p (k c) -> p k c", k=NCH)
                            mm(f, wv[:, k, 0:128], xn[:, k, 0:tw], k == 0, k == NCH - 1, [("ring", sl2[f]), ("xn", k)],
                               out_ap=ps[f][:, 0:tw])
                            mm(2 + f, wv[:, k, 128:256], xn[:, k, 0:tw], k == 0, k == NCH - 1,
                               [("ring", sl2[f]), ("xn", k)], out_ap=ps[2 + f][:, 0:tw])
                for f in range(NFH):
                    bg, bu = f % 2, 2 + f % 2
                    if not (half == 0 and f < 2):
                        slot = load_unit("gu%d" % l, half * NFH + f)
                        wv = ring[:, slot, :].rearrange("p (k c) -> p k c", k=NCH)
                        for k in range(NCH):
                            mm(bg, wv[:, k, 0:128], xn[:, k, 0:tw], k == 0, k == NCH - 1, [("ring", slot), ("xn", k)],
                               out_ap=ps[bg][:, 0:tw])
                        for k in range(NCH):
                            mm(bu, wv[:, k, 128:256], xn[:, k, 0:tw], k == 0, k == NCH - 1, [("ring", slot), ("xn", k)],
                               out_ap=ps[bu][:, 0:tw])
                    flush_stat()
                    fi = nxt("fb", NFB)
                    act(fb[fi][:, 0:tw], ps[bg][:, 0:tw], AF.Silu, [("ps", bg)], [("fb", fi), ("ps", bg)])
                    S.add("dve", (lambda e, f=f, fi=fi, bu=bu: e.tensor_tensor(out=big[:, f, 0:tw], in0=ps[bu][:, 0:tw],
                                                                              in1=fb[fi][:, 0:tw], op=ALU.mult)),
                          reads=[("ps", bu), ("fb", fi)], writes=[("big", f), ("ps", bu)])
                for m in range(NCH):
                    slot = load_unit("dn%d" % l, half * NCH + m)
                    wv = ring[:, slot, 0:NFH * 128].rearrange("p (k c) -> p k c", k=NFH)
                    bank = 4 + m % 2
                    for k in range(NFH):
                        mm(bank, wv[:, k, :], big[:, k, 0:tw], k == 0, k == NFH - 1, [("ring", slot), ("big", k)],
                           out_ap=ps[bank][:, 0:tw])
                    if half == 0:
                        act(v[:, m, 0:tw], ps[bank][:, 0:tw], AF.Copy, [("ps", bank)], [("v", m), ("ps", bank)])
                    else:
                        S.add("dve", (lambda e, m=m, bank=bank: e.tensor_tensor(out=v[:, m, 0:tw], in0=ps[bank][:, 0:tw],
                                                                               in1=v[:, m, 0:tw], op=ALU.add)),
                              reads=[("ps", bank), ("v", m)], writes=[("v", m), ("ps", bank)])
                        i = nxt("sq", NSQ)
                        act(sqb[i][:, 0:tw], v[:, m, 0:tw], AF.Square, [("v", m)], [("sq", i)])
                        flush_stat()
                        pending_stat.append(stat_mm(7, i, tw, m == 0, m == NCH - 1, "sq"))
            post_norm_residual("g_ffn_post%d" % l, tw)

        def attn_kv(t, tw):
            def evac_k(bank, g):
                if t == 0:
                    act(kmeta[:, g, :], ps[bank][:, 0:NMETA], AF.Identity, [("ps", bank)], [("kmeta", g), ("ps", bank)],
                        bias=col("b_k", g))
                    act(kbuf[:, g, 3 * 128:4 * 128], ps[bank][:, T0 - 128:T0], AF.Identity, [("ps", bank)],
                        [("kbuf", g, 3), ("ps", bank)], bias=col("b_k", g))
                else:
                    s0 = 4 * (t % 2)
                    act(kbuf[:, g, s0 * 128:(s0 + 4) * 128], ps[bank][:, 0:TW], AF.Identity, [("ps", bank)],
                        [("kbuf", g, s0 + i) for i in range(4)] + [("ps", bank)], bias=col("b_k", g))
            proj_fm("qkv", 8, 4, lambda k: xn[:, k, 0:tw], lambda k: ("xn", k), tw, evac_k)
            slot = load_unit("qkv", 10)
            wv = ring[:, slot, :].rearrange("p (k c) -> p k c", k=NCH)
            if t == 0:
                blocks = [(T0 - 128, 128, 3, 0), (0, NMETA, None, 0), (0, NMETA, None, 32)]
            else:
                blocks = [(i * 128, 128, 4 * (t % 2) + i, 0) for i in range(4)]
            for bi, (o, n, vs, pr) in enumerate(blocks):
                bank = 4 + bi % 2
                for k in range(NCH):
                    mm(bank, xn[:, k, o:o + n], wv[:, k, :], k == 0, k == NCH - 1, [("ring", slot), ("xn", k)],
                       out_ap=ps[bank][pr:pr + n, 0:256])
                if vs is None:
                    S.add("dve", (lambda e, bank=bank, pr=pr: e.tensor_tensor(
                        out=vmeta[pr:pr + NMETA, :], in0=ps[bank][pr:pr + NMETA, 0:256],
                        in1=bv[pr:pr + NMETA, :], op=ALU.add)),
                        reads=[("ps", bank), "bv"], writes=["vmeta", ("ps", bank)])
                else:
                    S.add("dve", (lambda e, bank=bank, vs=vs: e.tensor_tensor(out=vbuf[:, vs, :], in0=ps[bank][:, 0:256],
                                                                             in1=bv[:, :], op=ALU.add)),
                          reads=[("ps", bank), "bv"], writes=[("vbuf", vs), ("ps", bank)])

        def attn_full(t):
            tw = TW
            rmsnorm_to_xn("g_mix_pre1", tw)

            def evac_q(bank, m):
                act(big[:, m, 0:tw], ps[bank][:, 0:tw], AF.Identity, [("ps", bank)], [("big", m), ("ps", bank)],
                    bias=col("bq_s", m), scale=0.125)
            proj_fm("qkv", 0, NCH, lambda k: xn[:, k, 0:tw], lambda k: ("xn", k), tw, evac_q, kouter=True)
            attn_kv(t, tw)
            steps = [(i, g, par) for i in range(4) for g in range(4) for par in range(2)]
            NS = len(steps)

            def emit_scores(n):
                i, g, par = steps[n]
                st = n % 2
                sc = 4 * (t % 2) + i
                sp_ = (sc - 1) % 8
                rows = slice(par * 64, par * 64 + 64)
                qs = slice(i * 128, (i + 1) * 128)
                qap = big[rows, 4 * g:4 * g + 4, qs]
                qres = [("big", 4 * g + j) for j in range(4)]
                mm(2 * st, kbuf[rows, g, sp_ * 128:(sp_ + 1) * 128], qap, True, False, [("kbuf", g, sp_)] + qres)
                mm(2 * st, ident[:], expb_p[:, par, 4 * g:4 * g + 4, :], False, True, ["ident", "expb_p"])
                mm(2 * st + 1, kbuf[rows, g, sc * 128:(sc + 1) * 128], qap, True, False, [("kbuf", g, sc)] + qres)
                mm(2 * st + 1, ident[:], expb_c[:, par, 4 * g:4 * g + 4, :], False, True, ["ident", "expb_c"])
                mr = 32 * st
                S.add("pe", lambda e: e.matmul(ps[4][mr:mr + NMETA, :], lhsT=kmeta[rows, g, :], rhs=qap, start=True, stop=True),
                      reads=[("kmeta", g)] + qres, writes=[("ps4", st)] + ([("ps", 4)] if n < 2 else []))

            def emit_softmax(n):
                i, g, par = steps[n]
                st = n % 2
                first = (t == 1 and i == 0)
                pis = []
                for kb in (0, 1):
                    bk = 2 * st + kb
                    pi = nxt("pb", NP)
                    act(pb[pi][:], ps[bk][:], AF.Exp, [("ps", bk)], [("pb", pi), ("ps", bk)])
                    if first and kb == 0:
                        S.add("dve", (lambda e, pi=pi: e.tensor_scalar(out=pb[pi][:], in0=pb[pi][:],
                                                                       scalar1=col("flag"), scalar2=None,
                                                                       op0=ALU.mult)),
                              reads=[("pb", pi)] + CONST, writes=[("pb", pi)])
                    pis.append(pi)
                mr = 32 * st
                mrows = slice(mr, mr + NMETA)
                ei = nxt("fb", NFB)
                act(fb[ei][mrows, :], ps[4][mrows, :], AF.Exp, [("ps4", st)],
                    [("fb", ei), ("ps4", st)] + ([("ps", 4)] if n >= NS - 2 else []))
                mi = nxt("pb", NP)
                if first:
                    xi = nxt("fb", NFB)
                    S.add("pool", (lambda e, xi=xi: e.dma_start(
                        out=fb[xi][mrows, :].rearrange("p (j q) -> p j q", j=4),
                        in_=mef_d[:, par, 4 * g:4 * g + 4, :])),
                        writes=[("fb", xi)], dma=("mef", xi))
                    act(fb[xi][mrows, :], fb[xi][mrows, :], AF.Exp, [("fb", xi)], [("fb", xi)])
                    S.add("dve", (lambda e, mi=mi, ei=ei, xi=xi: e.tensor_tensor(
                        out=pb[mi][mrows, :], in0=fb[ei][mrows, :], in1=fb[xi][mrows, :], op=ALU.mult)),
                        reads=[("fb", ei), ("fb", xi)], writes=[("pb", mi)])
                else:
                    S.add("dve", (lambda e, mi=mi, ei=ei: e.tensor_tensor(
                        out=pb[mi][mrows, :].rearrange("p (j q) -> p j q", j=4),
                        in0=fb[ei][mrows, :].rearrange("p (j q) -> p j q", j=4),
                        in1=mg[mrows, par * 16 + 4 * g:par * 16 + 4 * g + 4].unsqueeze(2).broadcast_to([NMETA, 4, 128]),
                        op=ALU.mult)),
                        reads=[("fb", ei), "mg"], writes=[("pb", mi)])
                return pis, mi

            def emit_pv(n, pis, mi):
                i, g, par = steps[n]
                st = n % 2
                sc = 4 * (t % 2) + i
                sp_ = (sc - 1) % 8
                rows = slice(par * 64, par * 64 + 64)
                qs = slice(i * 128, (i + 1) * 128)
                mrows = slice(32 * st, 32 * st + NMETA)
                vcols = slice(g * 64, (g + 1) * 64)
                nb = 5 + (i * 4 + g) % 2
                for bank, lt in ((nb, "v"), (7, "o")):
                    o_ap = ps[bank][rows, :]
                    l_p = vbuf[:, sp_, vcols] if lt == "v" else ones[:, 0:64]
                    l_c = vbuf[:, sc, vcols] if lt == "v" else ones[:, 0:64]
                    l_m = vmeta[mrows, vcols] if lt == "v" else ones[mrows, 0:64]
                    mm(bank, l_p, pb[pis[0]][:], True, False, [("vbuf", sp_), "ones", ("pb", pis[0])], out_ap=o_ap)
                    mm(bank, l_c, pb[pis[1]][:], False, False, [("vbuf", sc), "ones", ("pb", pis[1])], out_ap=o_ap)
                    mm(bank, l_m, pb[mi][mrows, :], False, True, ["vmeta", "ones", ("pb", mi)], out_ap=o_ap)
                if par == 1:
                    fi = nxt("fb", NFB)
                    S.add("dve", (lambda e, fi=fi: e.tensor_tensor(
                        out=fb[fi][:].rearrange("p (j q) -> p j q", j=4),
                        in0=ps[7][:].rearrange("p (j q) -> p j q", j=4),
                        in1=col("es", 4 * g, 4).unsqueeze(2).broadcast_to([128, 4, 128]), op=ALU.add)),
                        reads=[("ps", 7)] + CONST, writes=[("fb", fi), ("ps", 7)])
                    act(fb[fi][:], fb[fi][:], AF.Ln, [("fb", fi)], [("fb", fi)])
                    act(fb[fi][:], fb[fi][:], AF.Exp, [("fb", fi)], [("fb", fi)], scale=-1.0)
                    S.add("dve", (lambda e, fi=fi: e.tensor_tensor(
                        out=big[:, 4 * g:4 * g + 4, qs],
                        in0=ps[nb][:].rearrange("p (j q) -> p j q", j=4),
                        in1=fb[fi][:].rearrange("p (j q) -> p j q", j=4), op=ALU.mult)),
                        reads=[("ps", nb), ("fb", fi)], writes=[("big", 4 * g + j) for j in range(4)] + [("ps", nb)])

            emit_scores(0)
            for n in range(NS):
                if n + 1 < NS:
                    emit_scores(n + 1)
                pis, mi = emit_softmax(n)
                emit_pv(n, pis, mi)
            proj_fm("wo", 0, NCH, lambda k: big[:, k, 0:tw], lambda k: ("big", k), tw,
                    lambda bank, m: evac_v_and_stat(bank, m, tw, col("b_o", m), m == 0, m == NCH - 1))
            post_norm_residual("g_mix_post1", tw)

        allh = [("h", c) for c in range(NCH)]
        for t in range(NT + 1):
            S.seg = t + 1
            tw = T0 if t == 0 else TW
            off = 0 if t == 0 else T0 + (t - 1) * TW
            for c4 in range(0, NCH, 4):
                S.add("sp", (lambda e, off=off, tw=tw, c4=c4: e.dma_start(out=h[:, c4:c4 + 4, 0:tw],
                                                                          in_=xin[:, c4:c4 + 4, off:off + tw])),
                      writes=[("h", c) for c in range(c4, c4 + 4)], dma=("xin", c4))
            conv_mixer(t, tw)
            ffn(0, tw)
            if t == 0:
                emit_casts(LATE, extra_reads=[("h", 0)])
                rmsnorm_to_xn("g_mix_pre1", tw)
                attn_kv(0, tw)
            else:
                attn_full(t)
                ffn(1, tw)
                for c4 in range(0, NCH, 4):
                    S.add("pool", (lambda e, t=t, c4=c4: e.dma_start(out=out_d[:, c4:c4 + 4, (t - 1) * TW:t * TW],
                                                                     in_=h[:, c4:c4 + 4, :])),
                          reads=[("h", c) for c in range(c4, c4 + 4)], dma=("out", c4))
        S.add("sp", lambda e: None, reads=allh + ["outdone"], writes=allh)
        S.emit(nc, es)
    return nc


def _t5_bucket(dist):
    dist = np.asarray(dist, np.int64)
    d = np.maximum(dist, 16).astype(np.float32)
    large = 16 + (np.log(d / np.float32(16.0)) / np.float32(math.log(8.0)) * np.float32(16.0)).astype(np.int32)
    return np.where(dist < 16, dist, np.minimum(large, 31)).astype(np.int64)


def _vec_cols(vv):
    return np.ascontiguousarray(np.asarray(vv, np.float32).reshape(-1, 128).T)


def _img(w, ncols):
    k = w.shape[0] // 128
    return np.ascontiguousarray(w.reshape(k, 128, ncols).transpose(1, 0, 2)).reshape(128, k * ncols)


def _prep_shared(inp):
    f = lambda k: np.asarray(inp[k], np.float32)
    sh = {}
    w_in = f("conv_w_in")[0]
    sh["w_cin"] = np.stack([_img(np.concatenate([w_in[:, c * 128:(c + 1) * 128],
                                                 w_in[:, D + c * 128:D + (c + 1) * 128]], 1), 256) for c in range(16)])
    wdw_ = f("conv_w_dw")[0]
    w_out = f("conv_w_out")[0]
    sh["w_cout"] = np.stack([_img(w_out[:, u * 256:(u + 1) * 256], 256) for u in range(8)])
    for l in range(2):
        wg, wu, wd = f("ffn_w_gate")[l], f("ffn_w_up")[l], f("ffn_w_down")[l]
        sh["w_gu%d" % l] = np.stack([_img(np.concatenate([wg[:, i * 128:(i + 1) * 128],
                                                          wu[:, i * 128:(i + 1) * 128]], 1), 256) for i in range(NF)])
        sh["w_dn%d" % l] = np.stack([_img(wd[half * NFH * 128:(half + 1) * NFH * 128, m * 128:(m + 1) * 128], 128)
                                     for half in range(2) for m in range(16)])
    wqkv = f("attn_w_qkv")[0]
    units = [_img(wqkv[:, u * 256:(u + 1) * 256], 256) for u in range(8)]
    wk = wqkv[:, D:D + 256]
    wv = wqkv[:, D + 256:D + 512]
    for u in range(2):
        kv0, kv1 = 2 * u, 2 * u + 1
        units.append(_img(np.concatenate([wk[:, kv0 * 64:(kv0 + 1) * 64]] * 2 + [wk[:, kv1 * 64:(kv1 + 1) * 64]] * 2, 1), 256))
    units.append(_img(wv, 256))
    sh["w_qkv"] = np.stack(units)
    wo = f("attn_w_o")[0]
    sh["w_wo"] = np.stack([_img(wo[:, u * 256:(u + 1) * 256], 256) for u in range(8)])

    cvt = np.zeros((128, NCV), np.float32)

    def put(name, arr):
        arr = np.asarray(arr, np.float32)
        cvt[:, _cols[name]:_cols[name] + arr.shape[1]] = arr
    for l in range(2):
        put("g_mix_pre%d" % l, _vec_cols(f("norm_mix_pre")[l]))
        put("g_mix_post%d" % l, _vec_cols(f("norm_mix_post")[l]))
        put("g_ffn_pre%d" % l, _vec_cols(f("norm_ffn_pre")[l]))
        put("g_ffn_post%d" % l, _vec_cols(f("norm_ffn_post")[l]))
    b_in = f("conv_b_in")[0]
    put("b_in_a", _vec_cols(b_in[:D]))
    put("b_in_g", _vec_cols(b_in[D:]))
    put("b_dw", _vec_cols(f("conv_b_dw")[0]))
    put("ln_g", _vec_cols(f("conv_ln_g")[0]))
    put("ln_b", _vec_cols(f("conv_ln_b")[0]))
    put("b_out", _vec_cols(f("conv_b_out")[0]))
    bqkv = f("attn_b_qkv")[0]
    put("b_q", _vec_cols(bqkv[:D]))
    put("b_o", _vec_cols(f("attn_b_o")[0]))
    bk = bqkv[D:D + 256].reshape(4, 64)
    put("b_k", np.concatenate([bk, bk], 1).T)
    sinks = f("attn_sinks")[0]
    put("sinks", np.stack([sinks[2 * np.arange(16) + (1 if p >= 64 else 0)] for p in range(128)]))
    wdw = f("conv_w_dw")[0]
    put("w_dw", np.concatenate([_vec_cols(wdw[j]) for j in range(CW)], 1))
    sh["cv"] = cvt
    sh["bv_rep"] = np.ascontiguousarray(np.broadcast_to(bqkv[D + 256:D + 512][None, :], (128, 256)))
    rb = f("rel_bias")
    sh["mg"] = np.ascontiguousarray(np.broadcast_to(
        np.concatenate([rb[31, 0::2], rb[31, 1::2]])[None, :], (16, 32)))
    kk = np.arange(128)[:, None]
    qq = np.arange(128)[None, :]
    hd = (2 * np.arange(16)[None, :] + np.arange(2)[:, None])

    def table(dist, valid):
        g = rb[_t5_bucket(np.maximum(dist, 0))]
        tab = g[:, :, hd]
        tab = np.where(valid[:, :, None, None], tab, np.float32(NEG))
        return np.ascontiguousarray(tab.transpose(0, 2, 3, 1)).reshape(dist.shape[0], -1).astype(np.float32)
    sh["ident"] = np.eye(128, dtype=np.float32)
    sh["bias_prev"] = table(128 + qq - kk, kk > qq)
    sh["bias_cur"] = table(qq - kk, kk <= qq)
    mm_ = np.arange(NMETA)[:, None]
    sh["_mef_first"] = table(NMETA + qq - mm_, np.ones((NMETA, 128), bool)).reshape(NMETA, 2, 16, 128)
    sh["_mef_gen"] = table(np.full((NMETA, 128), 1000), np.ones((NMETA, 128), bool)).reshape(NMETA, 2, 16, 128)
    return sh


def _run(inp, cores_per_batch, nt):
    x = np.asarray(inp["x"], np.float32)
    B, S_, _ = x.shape
    assert S_ == cores_per_batch * nt * TW
    meta = np.asarray(inp["meta_tokens"], np.float32)
    sh = _prep_shared(inp)
    mef_first = sh.pop("_mef_first")
    mef_gen = sh.pop("_mef_gen")
    in_maps = []
    for b in range(B):
        for q in range(cores_per_batch):
            s0 = q * nt * TW
            if q == 0:
                halo = np.concatenate([np.zeros((HALO - NMETA, D), np.float32), meta], 0)
            else:
                halo = x[b, s0 - HALO:s0]
            stream = np.concatenate([meta, halo, x[b, s0:s0 + nt * TW]], 0)
            ntok = stream.shape[0]
            xin = np.ascontiguousarray(stream.T.reshape(16, 128, ntok).transpose(1, 0, 2))
            mask0 = np.ones((128, T0), np.float32)
            cvt = sh["cv"].copy()
            if q == 0:
                mask0[:, NMETA:T0 - NMETA] = 0.0
                cvt[:, _cols["flag"]] = 0.0
            else:
                cvt[:, _cols["flag"]] = 1.0
            m = dict(sh)
            m["cv"] = cvt
            m["xin"] = xin
            m["mask0"] = mask0
            m["mef"] = np.ascontiguousarray(mef_first if q == 0 else mef_gen)
            in_maps.append(m)
    nc = build_program(nt)
    res = run_bass_kernel_spmd(nc, in_maps, core_ids=list(range(len(in_maps))))
    out = np.empty((B, S_, D), np.float32)
    i = 0
    for b in range(B):
        for q in range(cores_per_batch):
            o = np.asarray(res.results[i]["out"])
            out[b, q * nt * TW:(q + 1) * nt * TW] = o.transpose(2, 1, 0).reshape(nt * TW, D)
            i += 1
    return out


def kernel(**inputs):
    return _run(inputs, cores_per_batch=4, nt=8)
```
